# Optimizing a Trainium2 kernel written in Bass

```python
import jax, jax.numpy as jnp
from jax import lax
import numpy as np

D_MODEL = 4096
BATCH = 2
SEQ = 8192
DEPTH = 1

GRID_W = 64
CTX_LEN = 256
MIX_W = D_MODEL // 2
RET_DK = 256
RET_DV = 256
RET_HEADS = MIX_W // RET_DK
RET_CHUNK = 128
RET_THETA_BASE = 10000.0
ATT_DH = 128
ATT_HEADS = MIX_W // ATT_DH
ATT_KV_HEADS = ATT_HEADS // 4
ATT_GROUP = ATT_HEADS // ATT_KV_HEADS
WINDOW = 128
ATT_BLOCK = 128
ROPE_BASE = 10000.0
D_FF = 256 * ((8 * D_MODEL + 3 * 256 - 1) // (3 * 256))
EPS = 1e-6
NEG_INF = -1e30
RET_QK_W = RET_HEADS * RET_DK
RET_V_W = RET_HEADS * RET_DV
ATT_Q_W = ATT_HEADS * ATT_DH
ATT_KV_W = ATT_KV_HEADS * ATT_DH
IN_SPLITS = (RET_QK_W, RET_QK_W, RET_V_W, RET_V_W, ATT_Q_W, ATT_KV_W, ATT_KV_W, D_MODEL, D_MODEL)
IN_COLS = sum(IN_SPLITS)
IN_OFFSETS = tuple(int(o) for o in np.cumsum(IN_SPLITS)[:-1])

kernel_name = "hybrid_retention_window_gqa_macaron_dit"


def _rms_norm(x, g):
    xf = x.astype(jnp.float32)
    y = xf * lax.rsqrt(jnp.mean(xf * xf, axis=-1, keepdims=True) + EPS)
    return (y * g.astype(jnp.float32)).astype(x.dtype)


def _modulate(x, shift, scale):
    return x * (1 + scale) + shift


def _swiglu(x, w_gu, w_dn):
    a, u = jnp.split(x @ w_gu, 2, axis=-1)
    return (jax.nn.silu(a) * u) @ w_dn


def _ffn_sublayer(t, shift, scale, gate, g_pre, g_post, w_gu, w_dn):
    y = _swiglu(_modulate(_rms_norm(t, g_pre), shift, scale), w_gu, w_dn)
    return t + 0.5 * gate * _rms_norm(y, g_post)


def _rotate(x, cos, sin):
    x1, x2 = jnp.split(x, 2, axis=-1)
    return jnp.concatenate([x1 * cos - x2 * sin, x2 * cos + x1 * sin], axis=-1)


def _axial_rope(x):
    n = x.shape[1]
    rows = n // GRID_W
    row = jnp.repeat(jnp.arange(rows, dtype=jnp.float32), GRID_W)
    col = jnp.tile(jnp.arange(GRID_W, dtype=jnp.float32), rows)
    nf = ATT_DH // 4
    inv = ROPE_BASE ** (-jnp.arange(nf, dtype=jnp.float32) / nf)
    ang_r = row[:, None, None] * inv
    ang_c = col[:, None, None] * inv
    xr, xc = jnp.split(x, 2, axis=-1)
    xr = _rotate(xr, jnp.cos(ang_r).astype(x.dtype), jnp.sin(ang_r).astype(x.dtype))
    xc = _rotate(xc, jnp.cos(ang_c).astype(x.dtype), jnp.sin(ang_c).astype(x.dtype))
    return jnp.concatenate([xr, xc], axis=-1)


def _retention_rotate(x):
    n = x.shape[1]
    theta = 1.0 / RET_THETA_BASE ** jnp.linspace(0.0, 1.0, RET_DK // 2, dtype=jnp.float32)
    ang = jnp.arange(n, dtype=jnp.float32)[:, None, None] * theta
    return _rotate(x, jnp.cos(ang).astype(x.dtype), jnp.sin(ang).astype(x.dtype))


def _retention_chunkwise(q, k, v, log_g, s0):
    b, n, h, _ = q.shape
    dv = v.shape[-1]
    nc = n // RET_CHUNK

    def to_chunks(t):
        return t.astype(jnp.float32).reshape(b, nc, RET_CHUNK, h, t.shape[-1]).transpose(1, 0, 3, 2, 4)

    qc, kc, vc = to_chunks(q), to_chunks(k), to_chunks(v)
    idx = jnp.arange(RET_CHUNK, dtype=jnp.float32)
    lg = log_g[:, None]
    diff = idx[:, None] - idx[None, :]
    decay = jnp.where(diff >= 0, jnp.exp(lg[:, :, None] * jnp.maximum(diff, 0.0)), 0.0)
    q_dec = jnp.exp(lg * (idx + 1.0))[None, :, :, None]
    k_dec = jnp.exp(lg * (RET_CHUNK - 1.0 - idx))[None, :, :, None]
    c_dec = jnp.exp(log_g * RET_CHUNK)[None, :, None, None]

    def step(s, inp):
        qi, ki, vi = inp
        att = jnp.einsum('bhid,bhjd->bhij', qi, ki) * decay
        o = jnp.einsum('bhij,bhjv->bhiv', att, vi) + jnp.einsum('bhid,bhdv->bhiv', qi, s) * q_dec
        s = s * c_dec + jnp.einsum('bhjd,bhjv->bhdv', ki * k_dec, vi)
        return s, o

    s_fin, o = lax.scan(step, s0, (qc, kc, vc))
    o = o.transpose(1, 0, 3, 2, 4).reshape(b, n, h, dv)
    return o, s_fin


def _retention_output(o, g, gn):
    b, n = o.shape[:2]
    y = o * lax.rsqrt(jnp.mean(o * o, axis=-1, keepdims=True) + EPS)
    y = y.reshape(b, n, RET_V_W) * gn.astype(jnp.float32)
    return jax.nn.silu(g) * y.astype(g.dtype)


def _windowed_attention(q, k, v, kc, vc, sink):
    b, n = q.shape[:2]
    cl = kc.shape[1]
    nb = n // ATT_BLOCK
    scale = ATT_DH ** -0.5
    qb = q.reshape(b, nb, ATT_BLOCK, ATT_KV_HEADS, ATT_GROUP, ATT_DH)
    pad = ((0, 0), (ATT_BLOCK, ATT_BLOCK), (0, 0), (0, 0))

    def band(t):
        tb = jnp.pad(t, pad).reshape(b, nb + 2, ATT_BLOCK, ATT_KV_HEADS, ATT_DH)
        return jnp.concatenate([tb[:, :-2], tb[:, 1:-1], tb[:, 2:]], axis=2)

    kw, vw = band(k), band(v)
    wlen = 3 * ATT_BLOCK
    a = jnp.arange(ATT_BLOCK)[:, None]
    w = jnp.arange(wlen)[None, :]
    rel = w - ATT_BLOCK - a
    key_pos = (jnp.arange(nb)[:, None, None] - 1) * ATT_BLOCK + w[None]
    valid = (jnp.abs(rel)[None] <= WINDOW) & (key_pos >= 0) & (key_pos < n)

    s_win = jnp.einsum('bnqkgd,bnwkd->bnkgqw', qb, kw).astype(jnp.float32) * scale
    s_win = jnp.where(valid[None, :, None, None], s_win, NEG_INF)
    s_ctx = jnp.einsum('bnqkgd,bckd->bnkgqc', qb, kc).astype(jnp.float32) * scale
    s_sink = jnp.broadcast_to(sink.astype(jnp.float32).reshape(1, 1, ATT_KV_HEADS, ATT_GROUP, 1, 1),
                              s_ctx.shape[:-1] + (1,))
    p = jax.nn.softmax(jnp.concatenate([s_win, s_ctx, s_sink], axis=-1), axis=-1).astype(v.dtype)
    o = (jnp.einsum('bnkgqw,bnwkd->bnqkgd', p[..., :wlen], vw)
         + jnp.einsum('bnkgqc,bckd->bnqkgd', p[..., wlen:wlen + cl], vc))
    return o.reshape(b, n, ATT_Q_W)


def _context_attention(qc, kc, vc, sink):
    b, cl = qc.shape[:2]
    qg = qc.reshape(b, cl, ATT_KV_HEADS, ATT_GROUP, ATT_DH)
    s = jnp.einsum('bqkgd,bckd->bkgqc', qg, kc).astype(jnp.float32) * ATT_DH ** -0.5
    s_sink = jnp.broadcast_to(sink.astype(jnp.float32).reshape(1, ATT_KV_HEADS, ATT_GROUP, 1, 1),
                              s.shape[:-1] + (1,))
    p = jax.nn.softmax(jnp.concatenate([s, s_sink], axis=-1), axis=-1).astype(vc.dtype)
    o = jnp.einsum('bkgqc,bckd->bqkgd', p[..., :cl], vc)
    return o.reshape(b, cl, ATT_Q_W)


def _token_mixer(xl, xc, with_ctx, w_in, ret_log_rate, ret_gn, attn_sink, w_ret_up, w_att_up, w_out):
    b, n, _ = xl.shape
    cl = xc.shape[1]
    rq, rk, rv, rg, aq, ak, av, ga, gb = jnp.split(xl @ w_in, IN_OFFSETS, axis=-1)
    crq, crk, crv, crg, caq, cak, cav, cga, cgb = jnp.split(xc @ w_in, IN_OFFSETS, axis=-1)

    log_g_f = -jnp.exp(ret_log_rate[0].astype(jnp.float32))
    log_g_b = -jnp.exp(ret_log_rate[1].astype(jnp.float32))
    k_scale = RET_DK ** -0.5
    rql = _retention_rotate(rq.reshape(b, n, RET_HEADS, RET_DK))
    rkl = _retention_rotate(rk.reshape(b, n, RET_HEADS, RET_DK)) * k_scale
    rvl = rv.reshape(b, n, RET_HEADS, RET_DV)
    rqc = crq.reshape(b, cl, RET_HEADS, RET_DK)
    rkc = crk.reshape(b, cl, RET_HEADS, RET_DK) * k_scale
    rvc = crv.reshape(b, cl, RET_HEADS, RET_DV)
    s0 = jnp.zeros((b, RET_HEADS, RET_DK, RET_DV), jnp.float32)
    oc_f, s_f = _retention_chunkwise(rqc, rkc, rvc, log_g_f, s0)
    oc_b, s_b = _retention_chunkwise(rqc[:, ::-1], rkc[:, ::-1], rvc[:, ::-1], log_g_b, s0)
    ol_f, _ = _retention_chunkwise(rql, rkl, rvl, log_g_f, s_f)
    ol_b, _ = _retention_chunkwise(rql[:, ::-1], rkl[:, ::-1], rvl[:, ::-1], log_g_b, s_b)
    ret_l = _retention_output(ol_f + ol_b[:, ::-1], rg, ret_gn)

    aql = _axial_rope(aq.reshape(b, n, ATT_HEADS, ATT_DH))
    akl = _axial_rope(ak.reshape(b, n, ATT_KV_HEADS, ATT_DH))
    avl = av.reshape(b, n, ATT_KV_HEADS, ATT_DH)
    akc = cak.reshape(b, cl, ATT_KV_HEADS, ATT_DH)
    avc = cav.reshape(b, cl, ATT_KV_HEADS, ATT_DH)
    att_l = _windowed_attention(aql, akl, avl, akc, avc, attn_sink)

    yl = (jax.nn.sigmoid(ga) * (ret_l @ w_ret_up) + jax.nn.sigmoid(gb) * (att_l @ w_att_up)) @ w_out
    if with_ctx:
        ret_c = _retention_output(oc_f + oc_b[:, ::-1], crg, ret_gn)
        att_c = _context_attention(caq.reshape(b, cl, ATT_HEADS, ATT_DH), akc, avc, attn_sink)
        yc = (jax.nn.sigmoid(cga) * (ret_c @ w_ret_up) + jax.nn.sigmoid(cgb) * (att_c @ w_att_up)) @ w_out
        return yl, yc
    return yl, None


def setup_inputs(seed: int = 0) -> dict:
    key = jax.random.key(seed)
    ks = jax.random.split(key, 20)
    f32 = jnp.float32

    def nrm(k, shape, scale):
        return jax.random.normal(k, shape, f32) * scale

    base_rate = jnp.log(-jnp.log1p(-jnp.exp2(-5.0 - jnp.arange(RET_HEADS, dtype=f32))))
    return {
        "x": nrm(ks[0], (BATCH, SEQ, D_MODEL), 1.0),
        "c": nrm(ks[1], (BATCH, D_MODEL), 1.0),
        "ctx": nrm(ks[2], (BATCH, CTX_LEN, D_MODEL), 1.0),
        "c_ctx": nrm(ks[3], (D_MODEL,), 1.0),
        "w_ada": nrm(ks[4], (DEPTH, D_MODEL, 9 * D_MODEL), 0.5 * D_MODEL ** -0.5),
        "b_ada": nrm(ks[5], (DEPTH, 9 * D_MODEL), 0.01),
        "norm_pre": 1.0 + nrm(ks[6], (DEPTH, 3, D_MODEL), 0.05),
        "norm_post": 1.0 + nrm(ks[7], (DEPTH, 3, D_MODEL), 0.05),
        "ffn1_w_gu": nrm(ks[8], (DEPTH, D_MODEL, 2 * D_FF), D_MODEL ** -0.5),
        "ffn1_w_dn": nrm(ks[9], (DEPTH, D_FF, D_MODEL), D_FF ** -0.5),
        "ffn2_w_gu": nrm(ks[10], (DEPTH, D_MODEL, 2 * D_FF), D_MODEL ** -0.5),
        "ffn2_w_dn": nrm(ks[11], (DEPTH, D_FF, D_MODEL), D_FF ** -0.5),
        "w_in": nrm(ks[12], (DEPTH, D_MODEL, IN_COLS), D_MODEL ** -0.5),
        "ret_log_rate": base_rate[None, None, :] + nrm(ks[13], (DEPTH, 2, RET_HEADS), 0.1),
        "ret_gn": 1.0 + nrm(ks[14], (DEPTH, RET_V_W), 0.05),
        "attn_sink": nrm(ks[15], (DEPTH, ATT_HEADS), 0.5),
        "w_ret_up": nrm(ks[16], (DEPTH, RET_V_W, D_MODEL), RET_V_W ** -0.5),
        "w_att_up": nrm(ks[17], (DEPTH, ATT_Q_W, D_MODEL), ATT_Q_W ** -0.5),
        "w_out": nrm(ks[18], (DEPTH, D_MODEL, D_MODEL), D_MODEL ** -0.5),
    }


def reference(x, c, ctx, c_ctx, w_ada, b_ada, norm_pre, norm_post, ffn1_w_gu, ffn1_w_dn, ffn2_w_gu, ffn2_w_dn,
              w_in, ret_log_rate, ret_gn, attn_sink, w_ret_up, w_att_up, w_out):
    h, hc = x, ctx
    for layer in range(DEPTH):
        with_ctx = layer < DEPTH - 1
        m = jnp.split((jax.nn.silu(c) @ w_ada[layer] + b_ada[layer])[:, None, :], 9, axis=-1)
        mc = jnp.split((jax.nn.silu(c_ctx) @ w_ada[layer] + b_ada[layer])[None, None, :], 9, axis=-1)
        gpre, gpost = norm_pre[layer], norm_post[layer]

        h = _ffn_sublayer(h, m[0], m[1], m[2], gpre[0], gpost[0], ffn1_w_gu[layer], ffn1_w_dn[layer])
        hc = _ffn_sublayer(hc, mc[0], mc[1], mc[2], gpre[0], gpost[0], ffn1_w_gu[layer], ffn1_w_dn[layer])

        xl = _modulate(_rms_norm(h, gpre[1]), m[3], m[4])
        xc = _modulate(_rms_norm(hc, gpre[1]), mc[3], mc[4])
        yl, yc = _token_mixer(xl, xc, with_ctx, w_in[layer], ret_log_rate[layer], ret_gn[layer],
                              attn_sink[layer], w_ret_up[layer], w_att_up[layer], w_out[layer])
        h = h + m[5] * _rms_norm(yl, gpost[1])

        h = _ffn_sublayer(h, m[6], m[7], m[8], gpre[2], gpost[2], ffn2_w_gu[layer], ffn2_w_dn[layer])
        if with_ctx:
            hc = hc + mc[5] * _rms_norm(yc, gpost[1])
            hc = _ffn_sublayer(hc, mc[6], mc[7], mc[8], gpre[2], gpost[2], ffn2_w_gu[layer], ffn2_w_dn[layer])
    return h
```

```python
import contextlib
import numpy as np
import concourse.bass as bass
import concourse.mybir as mybir
from concourse.bass_utils import run_bass_kernel_spmd

F32 = mybir.dt.float32
BF16 = mybir.dt.bfloat16
AF = mybir.ActivationFunctionType
ALU = mybir.AluOpType
ENGS = ("pe", "act", "dve", "pool", "sp")
EPS = 1e-6
NEG = -1e30


class Op:
    __slots__ = ("eng", "fn", "reads", "writes", "kind", "semkey", "deps",
                 "signal", "ticket", "idx", "is_barrier")

    def __init__(self, eng, fn, reads, writes, kind, semkey=None):
        self.eng = eng
        self.fn = fn
        self.reads = reads
        self.writes = writes
        self.kind = kind
        self.semkey = semkey
        self.deps = ()
        self.signal = False
        self.ticket = 0
        self.is_barrier = False


class Prog:
    def __init__(self, nc, dry=False):
        self.nc = nc
        self.dry = dry
        self.ops = []
        self.last_w = {}
        self.readers = {}
        self.last_on_eng = {e: None for e in ENGS}
        self.last_dma_on_key = {}

    def _rec(self, op):
        if self.dry:
            return op
        deps = set()
        lw = self.last_w
        rd = self.readers
        for r in op.reads:
            p = lw.get(r)
            if p is not None:
                deps.add(p)
            rd.setdefault(r, []).append(op)
        for w in op.writes:
            p = lw.get(w)
            if p is not None:
                deps.add(p)
            rl = rd.get(w)
            if rl:
                deps.update(rl)
            lw[w] = op
            rd[w] = []
        deps.discard(op)
        op.deps = deps
        op.idx = len(self.ops)
        self.ops.append(op)
        self.last_on_eng[op.eng] = op
        if op.kind == "d":
            self.last_dma_on_key[op.semkey] = op
        return op

    def op(self, eng, fn, reads=(), writes=()):
        return self._rec(Op(eng, fn, tuple(reads), tuple(writes), "c"))

    def dma(self, eng, out, in_, reads=(), writes=(), semkey=None):
        assert semkey is not None

        def fn(e, out=out, in_=in_):
            return e.dma_start(out=out, in_=in_)

        return self._rec(Op(eng, fn, tuple(reads), tuple(writes), "d", semkey))

    def custom_dma(self, eng, fn, reads=(), writes=(), semkey=None):
        return self._rec(Op(eng, fn, tuple(reads), tuple(writes), "d", semkey))

    def barrier(self, engs=ENGS):
        if self.dry:
            return
        pend = [o for o in self.last_on_eng.values() if o is not None and o.kind == "c"]
        pend += list(self.last_dma_on_key.values())
        for e in engs:
            o = Op(e, None, (), (), "c")
            o.is_barrier = True
            o.deps = set(pend)
            o.idx = len(self.ops)
            self.ops.append(o)

    def emit(self, final_wait_eng="sp"):
        nc = self.nc
        ops = self.ops
        for o in ops:
            for p in o.deps:
                if p.kind == "c" and p.eng == "pe" and o.eng == "pe" and not o.is_barrier:
                    continue
                p.signal = True
        finals = [o for o in self.last_on_eng.values() if o is not None and o.kind == "c"]
        finals += list(self.last_dma_on_key.values())
        for o in finals:
            o.signal = True
        for o in ops:
            if o.kind == "d":
                o.signal = True
        cnt = {e: 0 for e in ENGS}
        dcnt = {}
        for o in ops:
            if o.is_barrier or not o.signal:
                continue
            if o.kind == "c":
                cnt[o.eng] += 1
                o.ticket = cnt[o.eng]
            else:
                dcnt[o.semkey] = dcnt.get(o.semkey, 0) + 16
                o.ticket = dcnt[o.semkey]
        self.stats = (dict(cnt), len(dcnt), len(ops))
        with contextlib.ExitStack() as st:
            esem = {e: st.enter_context(nc.semaphore("s_" + e)) for e in ENGS if e != "sp"}
            dsem = {}
            for k in dcnt:
                dsem[k] = st.enter_context(nc.semaphore("d_%d" % len(dsem)))
            block = st.enter_context(nc.Block())
            per_eng = {e: [o for o in ops if o.eng == e] for e in ENGS}

            def run(e, eng):
                waited = {}
                for o in per_eng[e]:
                    for p in sorted(o.deps, key=lambda q: q.idx):
                        if p.kind == "c":
                            if p.eng == "pe" and e == "pe" and not o.is_barrier:
                                continue
                            sem = esem[p.eng]
                            key = ("e", p.eng)
                        else:
                            sem = dsem[p.semkey]
                            key = ("d", p.semkey)
                        if waited.get(key, 0) >= p.ticket:
                            continue
                        waited[key] = p.ticket
                        eng.wait_ge(sem, p.ticket)
                    if o.is_barrier:
                        continue
                    ins = o.fn(eng)
                    if o.signal:
                        if o.kind == "c":
                            ins.then_inc(esem[o.eng], 1)
                        else:
                            ins.then_inc(dsem[o.semkey], 16)
                if e == final_wait_eng:
                    for p in finals:
                        if p.kind == "c":
                            if waited.get(("e", p.eng), 0) < p.ticket:
                                eng.wait_ge(esem[p.eng], p.ticket)
                        else:
                            if waited.get(("d", p.semkey), 0) < p.ticket:
                                eng.wait_ge(dsem[p.semkey], p.ticket)

            @block.tensor
            def _(eng):
                run("pe", eng)

            @block.scalar
            def _(eng):
                run("act", eng)

            @block.vector
            def _(eng):
                run("dve", eng)

            @block.gpsimd
            def _(eng):
                run("pool", eng)

            @block.sync
            def _(eng):
                run("sp", eng)


class Cfg:
    def __init__(self, D=4096, DFF=11008, SEQ=8192, B=2, CTX=256, GRID_W=64):
        self.D, self.DFF, self.SEQ, self.B, self.CTX, self.GRID_W = D, DFF, SEQ, B, CTX, GRID_W
        self.DC = D // 128
        self.FC = DFF // 128
        self.M = D // 2
        self.RH = self.M // 256
        self.AH = self.M // 128
        self.KVH = self.AH // 4
        self.T = SEQ // 4
        self.NB = self.T // 128
        self.NS = self.T // 512
        self.NF = 3 * self.NS
        self.NT = self.NS + 1 + self.NF
        self.NFB = 3 * self.NB
        M = self.M
        self.o_rq, self.o_rk, self.o_rv, self.o_rg, self.o_aq = 0, M, 2 * M, 3 * M, 4 * M
        self.o_ak = 5 * M
        self.o_av = 5 * M + M // 4
        self.o_ga = 5 * M + M // 2
        self.o_gb = self.o_ga + D
        self.INC = self.o_gb + D
        self.WSLOT = 4096
        self.NWS = 4


def fm_chunk_list(cfg):
    out = []
    perm = np.concatenate([np.arange(32, 64), np.arange(0, 32), np.arange(96, 128), np.arange(64, 96)])
    for h in range(cfg.RH):
        for e in range(2):
            out.append(("rq", (h, e), cfg.o_rq + h * 256 + e * 128 + np.arange(128)))
    for h in range(cfg.RH):
        for e in range(2):
            out.append(("rk", (h, e), cfg.o_rk + h * 256 + e * 128 + np.arange(128)))
    for c in range(2 * cfg.RH):
        out.append(("rg", c, cfg.o_rg + c * 128 + np.arange(128)))
    for h in range(cfg.AH):
        out.append(("aq", h, cfg.o_aq + h * 128 + np.arange(128)))
        out.append(("aqs", h, cfg.o_aq + h * 128 + perm))
    for h in range(cfg.KVH):
        out.append(("ak", h, cfg.o_ak + h * 128 + np.arange(128)))
        out.append(("aks", h, cfg.o_ak + h * 128 + perm))
    for c in range(cfg.DC):
        out.append(("ga", c, cfg.o_ga + c * 128 + np.arange(128)))
    for c in range(cfg.DC):
        out.append(("gb", c, cfg.o_gb + c * 128 + np.arange(128)))
    return out


def tm_block_list(cfg):
    out = []
    W = min(512, cfg.RH * 256)
    for i in range(cfg.RH * 256 // W):
        out.append(("rk", i, cfg.o_rk + i * W, W))
    for i in range(cfg.RH * 256 // W):
        out.append(("rv", i, cfg.o_rv + i * W, W))
    Wv = min(512, cfg.KVH * 128)
    for i in range(cfg.KVH * 128 // Wv):
        out.append(("av", i, cfg.o_av + i * Wv, Wv))
    return out


def fm_tiles(W, col_lists):
    K = W.shape[0]
    KC = K // 128
    out = np.empty((len(col_lists), 128, KC * 128), np.float32)
    for i, cols in enumerate(col_lists):
        blk = W[:, cols].reshape(KC, 128, 128)
        out[i] = blk.transpose(1, 0, 2).reshape(128, KC * 128)
    return out


def tm_tiles(W, col0, width, wslot):
    K = W.shape[0]
    KC = K // 128
    kcg = min(wslot // width, KC)
    npc = KC // kcg
    blk = W[:, col0:col0 + width].reshape(npc, kcg, 128, width)
    return np.ascontiguousarray(blk.transpose(0, 2, 1, 3).reshape(npc, 128, kcg * width)), kcg


def prep_host(inp, cfg):
    D, DC, T, NS, NB, RH, AH, KVH = cfg.D, cfg.DC, cfg.T, cfg.NS, cfg.NB, cfg.RH, cfg.AH, cfg.KVH
    f32 = np.float32
    x, c, ctx, c_ctx = inp["x"], inp["c"], inp["ctx"], inp["c_ctx"]
    shared = {}
    shared["wada"] = fm_tiles(inp["w_ada"][0], [np.arange(i * 128, (i + 1) * 128) for i in range(9 * DC)])
    for nm, gu, dn in (("1", "ffn1_w_gu", "ffn1_w_dn"), ("2", "ffn2_w_gu", "ffn2_w_dn")):
        cl = []
        for j in range(cfg.FC):
            cl.append(np.arange(j * 128, (j + 1) * 128))
            cl.append(cfg.DFF + np.arange(j * 128, (j + 1) * 128))
        shared["wgu" + nm] = fm_tiles(inp[gu][0], cl)
        shared["wdn" + nm] = fm_tiles(inp[dn][0], [np.arange(i * 128, (i + 1) * 128) for i in range(DC)])
    win = inp["w_in"][0]
    shared["winfm"] = fm_tiles(win, [cl for _, _, cl in fm_chunk_list(cfg)])
    for i, (kind, idx, c0, w) in enumerate(tm_block_list(cfg)):
        shared["wintm%d" % i], _ = tm_tiles(win, c0, w, cfg.WSLOT)
    shared["wrup"] = fm_tiles(inp["w_ret_up"][0], [np.arange(i * 128, (i + 1) * 128) for i in range(DC)])
    shared["waup"] = fm_tiles(inp["w_att_up"][0], [np.arange(i * 128, (i + 1) * 128) for i in range(DC)])
    shared["wout"] = fm_tiles(inp["w_out"][0], [np.arange(i * 128, (i + 1) * 128) for i in range(DC)])

    def fm_vec(v):
        return np.ascontiguousarray(v.reshape(-1, 128).T)

    shared["bada"] = fm_vec(inp["b_ada"][0])
    shared["gpre"] = np.ascontiguousarray(inp["norm_pre"][0].reshape(3, DC, 128).transpose(2, 0, 1).reshape(128, 3 * DC))
    shared["gpost"] = np.ascontiguousarray(inp["norm_post"][0].reshape(3, DC, 128).transpose(2, 0, 1).reshape(128, 3 * DC))
    shared["rate"] = np.ascontiguousarray(np.broadcast_to(inp["ret_log_rate"][0].reshape(1, 2 * RH), (128, 2 * RH))).astype(f32)
    shared["gn"] = fm_vec(inp["ret_gn"][0])
    shared["sink"] = np.ascontiguousarray(np.broadcast_to(inp["attn_sink"][0].reshape(1, AH), (128, AH))).astype(f32)
    ii = np.arange(128, dtype=f32)
    diff = ii[None, :] - ii[:, None]
    shared["dtab"] = np.concatenate([np.maximum(diff, 0), (diff >= 0).astype(f32),
                                     np.maximum(-diff, 0), (diff <= 0).astype(f32)], axis=1).astype(f32)
    tl = np.arange(T) % 128
    shared["qidx"] = np.ascontiguousarray(np.broadcast_to(
        np.concatenate([tl + 1.0, 128.0 - tl]).astype(f32)[None, :], (128, 2 * T)))
    shared["ident"] = np.eye(128, dtype=f32)
    shared["kcol"] = np.stack([127.0 - ii, ii, np.full(128, 128.0, f32)], axis=1).astype(f32)

    theta = (1.0 / 10000.0 ** np.linspace(0.0, 1.0, 128, dtype=f32)).astype(f32)
    nf = 32
    inv = (10000.0 ** (-np.arange(nf, dtype=f32) / nf)).astype(f32)

    per_core = []
    for core in range(8):
        b, j = core // 4, core % 4
        d = dict(shared)
        t0 = j * T
        NF, NT, NFB = cfg.NF, cfg.NT, cfg.NFB
        xt = np.zeros((NT, 512, D), f32)
        xt[:NS] = x[b, t0:t0 + T].reshape(NS, 512, D)
        if j > 0:
            xt[NS, 0:128] = x[b, t0 - 128:t0]
        if j < 3:
            xt[NS, 128:256] = x[b, t0 + T:t0 + T + 128]
        xt[NS, 256:512] = ctx[b]
        fpos = np.concatenate([np.arange(sg * T, (sg + 1) * T) for sg in range(4) if sg != j])
        xt[NS + 1:] = x[b, fpos].reshape(NF, 512, D)
        d["xT"] = np.ascontiguousarray(xt.reshape(NT, 512, DC, 128).transpose(0, 2, 3, 1))
        d["cT"] = np.ascontiguousarray(np.stack([c[b], c_ctx], axis=1).reshape(DC, 128, 2).transpose(1, 0, 2).reshape(128, DC * 2))
        pos_own = (t0 + np.arange(T)).astype(f32)
        pos_ext = np.concatenate([t0 - 128 + np.arange(128), t0 + T + np.arange(128)]).astype(f32)
        ang = pos_own[None, :] * theta[:, None]
        d["cosR"] = np.ascontiguousarray(np.cos(ang).astype(f32).reshape(128, NS, 512).transpose(1, 0, 2))
        d["sinR"] = np.ascontiguousarray(np.sin(ang).astype(f32).reshape(128, NS, 512).transpose(1, 0, 2))
        angt = pos_own[:, None] * theta[None, :]
        ct = np.ones((NT, 4, 128, 128), f32)
        sn = np.zeros((NT, 4, 128, 128), f32)
        ct[:NS] = np.cos(angt).astype(f32).reshape(NS, 4, 128, 128)
        sn[:NS] = np.sin(angt).astype(f32).reshape(NS, 4, 128, 128)
        angf = fpos.astype(f32)[:, None] * theta[None, :]
        ct[NS + 1:] = np.cos(angf).astype(f32).reshape(NF, 4, 128, 128)
        sn[NS + 1:] = np.sin(angf).astype(f32).reshape(NF, 4, 128, 128)
        d["cosRT"] = np.ascontiguousarray(np.concatenate([ct, ct], axis=3))
        d["sinRT"] = np.ascontiguousarray(np.concatenate([sn, sn], axis=3))
        pos_all = np.concatenate([pos_own, pos_ext])
        row = np.floor(pos_all / cfg.GRID_W).astype(f32)
        col = (pos_all - row * cfg.GRID_W).astype(f32)
        ar = row[None, :] * inv[:, None]
        ac = col[None, :] * inv[:, None]
        cr, sr, cc, sc = np.cos(ar).astype(f32), np.sin(ar).astype(f32), np.cos(ac).astype(f32), np.sin(ac).astype(f32)
        cosA = np.concatenate([cr, cr, cc, cc], axis=0)
        sinA = np.concatenate([-sr, sr, -sc, sc], axis=0)
        cA = np.ones((NS + 1, 128, 512), f32)
        sA = np.zeros((NS + 1, 128, 512), f32)
        cA[:NS] = cosA[:, :T].reshape(128, NS, 512).transpose(1, 0, 2)
        sA[:NS] = sinA[:, :T].reshape(128, NS, 512).transpose(1, 0, 2)
        cA[NS, :, :256] = cosA[:, T:]
        sA[NS, :, :256] = sinA[:, T:]
        d["cosA"], d["sinA"] = cA, sA
        a = np.arange(128)[:, None]
        w = np.arange(384)[None, :]
        band = np.abs(w - 128 - a) <= 128
        mk = []
        for which in range(3):
            v = band.copy()
            if which == 0 and j == 0:
                v &= (w >= 128)
            if which == 2 and j == 3:
                v &= (w < 256)
            mk.append(np.where(v, 0.0, NEG).astype(f32))
        d["amask"] = np.ascontiguousarray(np.concatenate(mk, axis=1))
        NE = NFB + 1
        E = np.zeros((2, NE), f32)
        Mk = np.zeros((2, NE), f32)
        gbs = np.concatenate([np.arange(sg * NB, (sg + 1) * NB) for sg in range(4) if sg != j])
        for ci, gb in enumerate(gbs):
            if gb < j * NB:
                E[0, ci] = 128.0 * (j * NB - 1 - gb)
                Mk[0, ci] = 1
            else:
                E[1, ci] = 128.0 * (gb - (j + 1) * NB)
                Mk[1, ci] = 1
        E[0, NFB], Mk[0, NFB] = 128.0 * j * NB, 1
        E[1, NFB], Mk[1, NFB] = 128.0 * (4 * NB - (j + 1) * NB), 1
        d["etab"] = np.ascontiguousarray(np.broadcast_to(np.concatenate([E.reshape(-1), Mk.reshape(-1)])[None, :], (128, 4 * NE))).astype(f32)
        per_core.append(d)
    return per_core


class WStream:
    def __init__(self, P, slots, plan=None):
        self.P = P
        self.slots = slots
        self.n = len(slots)
        self.plan = plan
        self.req = []
        self.issued = 0
        self.k = 0

    def get(self, src, ncols):
        k = self.k
        self.k += 1
        if self.plan is None:
            self.req.append((src, ncols))
            return self.slots[k % self.n][:, 0:ncols], ("wr", k % self.n)
        lim = min(k + self.n, len(self.plan))
        while self.issued < lim:
            i = self.issued
            s, nco = self.plan[i]
            sl = i % self.n
            self.P.dma("pool", self.slots[sl][:, 0:nco], s, writes=[("wr", sl)], semkey=("wr", sl))
            self.issued += 1
        return self.slots[k % self.n][:, 0:ncols], ("wr", k % self.n)


def build(cfg, upto=99, dbg=False):
    nc = bass.Bass("TRN2", target_bir_lowering=False)
    D, DC, FC, T, NS, NB, RH, AH, KVH = cfg.D, cfg.DC, cfg.FC, cfg.T, cfg.NS, cfg.NB, cfg.RH, cfg.AH, cfg.KVH
    NF, NT, NFB = cfg.NF, cfg.NT, cfg.NFB
    fmcl = fm_chunk_list(cfg)
    tmbl = tm_block_list(cfg)

    def din(name, shape, dt=F32):
        return nc.dram_tensor(name, list(shape), dt, kind="ExternalInput").ap()

    def dscr(name, shape, dt=F32):
        if dbg:
            return nc.dram_tensor(name, list(shape), dt, kind="ExternalOutput").ap()
        return nc.dram_tensor(name, list(shape), dt).ap()

    I = {}
    I["xT"] = din("xT", [NT, DC, 128, 512])
    I["cT"] = din("cT", [128, DC * 2])
    I["wada"] = din("wada", [9 * DC, 128, DC * 128])
    for nm in ("1", "2"):
        I["wgu" + nm] = din("wgu" + nm, [2 * FC, 128, DC * 128])
        I["wdn" + nm] = din("wdn" + nm, [DC, 128, FC * 128])
    I["winfm"] = din("winfm", [len(fmcl), 128, DC * 128])
    tm_kcg = []
    for i, (kind, idx, c0, w) in enumerate(tmbl):
        kcg = min(cfg.WSLOT // w, DC)
        tm_kcg.append(kcg)
        I["wintm%d" % i] = din("wintm%d" % i, [DC // kcg, 128, kcg * w])
    I["wrup"] = din("wrup", [DC, 128, 2 * RH * 128])
    I["waup"] = din("waup", [DC, 128, AH * 128])
    I["wout"] = din("wout", [DC, 128, DC * 128])
    for nm, n in (("bada", 9 * DC), ("gpre", 3 * DC), ("gpost", 3 * DC), ("rate", 2 * RH), ("gn", 2 * RH),
                  ("sink", AH), ("dtab", 512), ("qidx", 2 * T), ("kcol", 3), ("amask", 3 * 384), ("etab", 4 * (NFB + 1)),
                  ("ident", 128)):
        I[nm] = din(nm, [128, n])
    I["cosR"] = din("cosR", [NS, 128, 512])
    I["sinR"] = din("sinR", [NS, 128, 512])
    I["cosRT"] = din("cosRT", [NT, 4, 128, 256])
    I["sinRT"] = din("sinRT", [NT, 4, 128, 256])
    I["cosA"] = din("cosA", [NS + 1, 128, 512])
    I["sinA"] = din("sinA", [NS + 1, 128, 512])
    outT = nc.dram_tensor("outT", [NS, DC, 128, 512], F32, kind="ExternalOutput").ap()
    DBG = {}
    if dbg:
        DBG["sc"] = nc.dram_tensor("d_sc", [128, 9 * 2 * DC], F32, kind="ExternalOutput").ap()
        DBG["xm"] = nc.dram_tensor("d_xm", [128, DC * 512], BF16, kind="ExternalOutput").ap()
        DBG["ht"] = nc.dram_tensor("d_ht", [128, FC * 512], BF16, kind="ExternalOutput").ap()
        DBG["y"] = nc.dram_tensor("d_y", [128, DC * 512], F32, kind="ExternalOutput").ap()
        DBG["rs"] = nc.dram_tensor("d_rs", [128, 512], F32, kind="ExternalOutput").ap()

    S = {}
    S["h1"] = dscr("s_h1", [NS, DC, 128, 512])
    S["h2"] = dscr("s_h2", [NS, DC, 128, 512])
    S["xl"] = dscr("s_xl", [NT, DC, 128, 512], BF16)
    S["rq"] = dscr("s_rq", [RH, 2, 128, T], BF16)
    S["rk"] = dscr("s_rk", [RH, 2, 128, T], BF16)
    S["rg"] = dscr("s_rg", [2 * RH, 128, T])
    S["rktm"] = dscr("s_rktm", [NB + 4 + NFB, 128, RH * 256], BF16)
    S["rvtm"] = dscr("s_rvtm", [NB + 4 + NFB, 128, RH * 256], BF16)
    S["aq"] = dscr("s_aq", [AH, 128, T], BF16)
    S["ak"] = dscr("s_ak", [KVH, 128, T + 256], BF16)
    S["cak"] = dscr("s_cak", [KVH, 128, 256], BF16)
    S["av"] = dscr("s_av", [NB + 2, 128, KVH * 128], BF16)
    S["cav"] = dscr("s_cav", [2, 128, KVH * 128], BF16)
    S["ga"] = dscr("s_ga", [DC, 128, T])
    S["gb"] = dscr("s_gb", [DC, 128, T])
    S["retl"] = dscr("s_retl", [2 * RH, 128, T], BF16)
    S["attl"] = dscr("s_attl", [AH, 128, T], BF16)

    with contextlib.ExitStack() as st:
        NW = 53000
        big = st.enter_context(nc.sbuf_tensor("big", [128, NW], F32))
        psb = [st.enter_context(nc.psum_tensor("ps%d" % i, [128, 512], F32)) for i in range(7)]
        pst = st.enter_context(nc.psum_tensor("pst", [128, 1024], BF16))

        class Mem:
            def __init__(self, base, size):
                self.base, self.size, self.o = base, size, base

            def f32(self, n):
                a = big[:, self.o:self.o + n]
                self.o += n
                assert self.o <= self.base + self.size, (self.o, self.base, self.size)
                return a

            def bf(self, n):
                w = (n + 1) // 2
                a = big[:, self.o:self.o + w].bitcast(BF16)
                self.o += w
                assert self.o <= self.base + self.size, (self.o, self.base, self.size)
                return a[:, 0:n]

            def reset(self):
                self.o = self.base

        perm = Mem(0, 1400 + 9 * 2 * DC + 6 * (NFB + 1))
        SC = perm.f32(9 * 2 * DC)
        ONESM = perm.f32(128)
        LG = perm.f32(2 * RH)
        GNT = perm.f32(2 * RH)
        SINK = perm.f32(AH)
        KCOL = perm.f32(3)
        ETAB = perm.f32(4 * (NFB + 1))
        IDENT = perm.bf(128)
        HS = perm.f32(8)
        CO = perm.f32(2 * (NFB + 1))
        COLS = perm.f32(64)
        small = Mem(perm.base + perm.size, 4096)
        assert perm.o <= perm.size
        RS = [small.f32(512), small.f32(512)]
        ACC = small.f32(512)
        SQ = small.f32(1024)
        TMP = [small.f32(512) for _ in range(3)]
        wr0 = small.base + small.size
        WR = []
        for i in range(cfg.NWS):
            WR.append(big[:, wr0 + i * (cfg.WSLOT // 2): wr0 + (i + 1) * (cfg.WSLOT // 2)].bitcast(BF16))
        r1_0 = wr0 + cfg.NWS * cfg.WSLOT // 2
        R1 = Mem(r1_0, DC * 512)
        R2 = Mem(r1_0 + DC * 512, NW - (r1_0 + DC * 512))
        RM = Mem(r1_0, NW - r1_0)
        assert R2.size >= FC * 256, (R2.size, FC * 256)
        NOPOOL = ("pe", "act", "dve", "sp")

        def sc(k, var, fc):
            i = (k * 2 + var) * DC + fc
            return SC[:, i:i + 1]

        def v3(ap, a):
            return ap.rearrange("p (a b) -> p a b", a=a)

        def run_pass(P, ws):
            bank_rr = [0]

            def nbank(n=6):
                b = bank_rr[0] % n
                bank_rr[0] += 1
                return b

            def stats_rstd(chunks, rs_out, rs_res, div, ncols=512):
                n = len(chunks)
                first = True
                g = 0
                while g < n:
                    grp = chunks[g:g + 2]
                    g += 2
                    for i, (ap, r) in enumerate(grp):
                        P.op("act", lambda e, ap=ap, i=i: e.activation(out=SQ[:, i * 512:i * 512 + ncols], in_=ap, func=AF.Square),
                             reads=(list(r) if isinstance(r, list) else [r]), writes=[("sq", i)])
                    if len(grp) == 2:
                        if first:
                            P.op("dve", lambda e: e.tensor_tensor(out=ACC[:, 0:ncols], in0=SQ[:, 0:ncols], in1=SQ[:, 512:512 + ncols], op=ALU.add),
                                 reads=[("sq", 0), ("sq", 1)], writes=["acc"])
                        else:
                            P.op("dve", lambda e: e.tensor_tensor(out=SQ[:, 0:ncols], in0=SQ[:, 0:ncols], in1=SQ[:, 512:512 + ncols], op=ALU.add),
                                 reads=[("sq", 0), ("sq", 1)], writes=[("sq", 0)])
                            P.op("dve", lambda e: e.tensor_tensor(out=ACC[:, 0:ncols], in0=ACC[:, 0:ncols], in1=SQ[:, 0:ncols], op=ALU.add),
                                 reads=[("sq", 0), "acc"], writes=["acc"])
                    else:
                        if first:
                            P.op("dve", lambda e: e.tensor_copy(out=ACC[:, 0:ncols], in_=SQ[:, 0:ncols]), reads=[("sq", 0)], writes=["acc"])
                        else:
                            P.op("dve", lambda e: e.tensor_tensor(out=ACC[:, 0:ncols], in0=ACC[:, 0:ncols], in1=SQ[:, 0:ncols], op=ALU.add),
                                 reads=[("sq", 0), "acc"], writes=["acc"])
                    first = False
                P.op("pe", lambda e: e.matmul(psb[6][:, 0:ncols], ONESM, ACC[:, 0:ncols], start=True, stop=True),
                     reads=["acc", "ones"], writes=[("ps", 6)])
                P.op("dve", lambda e: e.tensor_scalar(out=rs_out[:, 0:ncols], in0=psb[6][:, 0:ncols], scalar1=1.0 / div, scalar2=EPS, op0=ALU.mult, op1=ALU.add),
                     reads=[("ps", 6)], writes=[rs_res])
                P.op("act", lambda e: e.activation(out=rs_out[:, 0:ncols], in_=rs_out[:, 0:ncols], func=AF.Sqrt), reads=[rs_res], writes=[rs_res])
                P.op("dve", lambda e: e.reciprocal(out=rs_out[:, 0:ncols], in_=rs_out[:, 0:ncols]), reads=[rs_res], writes=[rs_res])

            def stats_add(ap, res, i, first, add_eng):
                sl = i % 2
                sq = SQ[:, sl * 512:(sl + 1) * 512]
                P.op("act", lambda e: e.activation(out=sq, in_=ap, func=AF.Square),
                     reads=(list(res) if isinstance(res, list) else [res]), writes=[("sq", sl)])
                if first:
                    P.op(add_eng, lambda e: e.tensor_copy(out=ACC, in_=sq), reads=[("sq", sl)], writes=["acc"])
                else:
                    P.op(add_eng, lambda e: e.tensor_tensor(out=ACC, in0=ACC, in1=sq, op=ALU.add), reads=[("sq", sl), "acc"], writes=["acc"])

            def stats_finish(rs_out, rs_res, div):
                P.op("pe", lambda e: e.matmul(psb[6][:], ONESM, ACC, start=True, stop=True),
                     reads=["acc", "ones"], writes=[("ps", 6)])
                P.op("dve", lambda e: e.tensor_scalar(out=rs_out, in0=psb[6][:], scalar1=1.0 / div, scalar2=EPS, op0=ALU.mult, op1=ALU.add),
                     reads=[("ps", 6)], writes=[rs_res])
                P.op("act", lambda e: e.activation(out=rs_out, in_=rs_out, func=AF.Sqrt), reads=[rs_res], writes=[rs_res])
                P.op("dve", lambda e: e.reciprocal(out=rs_out, in_=rs_out), reads=[rs_res], writes=[rs_res])

            def mod_chunk(src, src_res, rs, rs_res, kA, kB, fc, segs, dst, dst_res, tmp_i):
                tm = TMP[tmp_i]
                P.op("dve", lambda e: e.tensor_tensor(out=tm, in0=src, in1=rs, op=ALU.mult),
                     reads=[src_res, rs_res], writes=[("tmp", tmp_i)])
                for (c0, c1, var) in segs:
                    P.op("act", lambda e, c0=c0, c1=c1, var=var: e.activation(
                        out=dst[:, c0:c1], in_=tm[:, c0:c1], func=AF.Identity, bias=sc(kB, var, fc), scale=sc(kA, var, fc)),
                        reads=[("tmp", tmp_i), "sc"], writes=[dst_res])

            def phase0():
                R1.reset()
                MR = R1.f32(9 * DC * 2)
                CT = R1.f32(DC * 2)
                SCT = R1.bf(DC * 2)
                BA = R1.f32(9 * DC)
                GP = R1.f32(3 * DC)
                GQ = R1.f32(3 * DC)
                RT = R1.f32(2 * RH)
                IDF = R1.f32(128)
                T1 = R1.f32(DC)
                for nm, dst in (("cT", CT), ("bada", BA), ("gpre", GP), ("gpost", GQ), ("rate", RT), ("gn", GNT),
                                ("sink", SINK), ("kcol", KCOL), ("etab", ETAB), ("ident", IDF)):
                    P.dma("sp", dst, I[nm], writes=[("c", nm)], semkey="c")
                P.barrier(NOPOOL)
                P.op("dve", lambda e: e.memset(ONESM, 1.0), writes=["ones"])
                P.op("dve", lambda e: e.tensor_copy(out=IDENT, in_=IDF), reads=[("c", "ident")], writes=["ident"])
                P.op("act", lambda e: e.activation(out=LG, in_=RT, func=AF.Exp), reads=[("c", "rate")], writes=["lg"])
                P.op("dve", lambda e: e.tensor_scalar(out=LG, in0=LG, scalar1=-1.0, scalar2=None, op0=ALU.mult), reads=["lg"], writes=["lg"])
                P.op("act", lambda e: e.activation(out=SCT, in_=CT, func=AF.Silu), reads=[("c", "cT")], writes=["sct"])
                noc = 9 * DC
                for oc in range(noc):
                    slot, sres = ws.get(I["wada"][oc], DC * 128)
                    bk, col = divmod(oc * 2, 512)

                    def mm(e, slot=slot, bk=bk, col=col):
                        for kc in range(DC):
                            ins = e.matmul(psb[bk][:, col:col + 2], slot[:, kc * 128:(kc + 1) * 128], SCT[:, kc * 2:kc * 2 + 2],
                                           start=(kc == 0), stop=(kc == DC - 1))
                        return ins
                    P.op("pe", mm, reads=[sres, "sct"], writes=[("ps", bk)])
                nb_used = (noc * 2 + 511) // 512
                for bk in range(nb_used):
                    n = min(512, noc * 2 - bk * 512)
                    P.op("act", lambda e, bk=bk, n=n: e.activation(out=MR[:, bk * 512:bk * 512 + n], in_=psb[bk][:, 0:n], func=AF.Identity),
                         reads=[("ps", bk)], writes=["mr"])
                MR4 = MR.rearrange("p (i f v) -> p i f v", i=9, f=DC, v=2)
                BA3 = BA.rearrange("p (i f) -> p i f", i=9)
                for var in range(2):
                    for i in range(9):
                        P.op("dve", lambda e, i=i, var=var: e.tensor_tensor(out=MR4[:, i, :, var], in0=MR4[:, i, :, var], in1=BA3[:, i, :], op=ALU.add),
                             reads=["mr", ("c", "bada")], writes=["mr"])
                    for k in range(3):
                        shift, scale, gate = MR4[:, 3 * k, :, var], MR4[:, 3 * k + 1, :, var], MR4[:, 3 * k + 2, :, var]
                        coef = 1.0 if k == 1 else 0.5
                        oA = ((3 * k) * 2 + var) * DC
                        oB = ((3 * k + 1) * 2 + var) * DC
                        oC = ((3 * k + 2) * 2 + var) * DC
                        P.op("dve", lambda e, scale=scale: e.tensor_scalar(out=T1, in0=scale, scalar1=1.0, scalar2=None, op0=ALU.add),
                             reads=["mr"], writes=["t1"])
                        P.op("dve", lambda e, k=k, oA=oA: e.tensor_tensor(out=SC[:, oA:oA + DC], in0=T1, in1=GP[:, k * DC:(k + 1) * DC], op=ALU.mult),
                             reads=["t1", ("c", "gpre")], writes=["sc"])
                        P.op("dve", lambda e, shift=shift, oB=oB: e.tensor_copy(out=SC[:, oB:oB + DC], in_=shift), reads=["mr"], writes=["sc"])
                        P.op("dve", lambda e, gate=gate, k=k, oC=oC, coef=coef: e.scalar_tensor_tensor(
                            out=SC[:, oC:oC + DC], in0=gate, scalar=coef, in1=GQ[:, k * DC:(k + 1) * DC], op0=ALU.mult, op1=ALU.mult),
                            reads=["mr", ("c", "gpost")], writes=["sc"])
                if dbg:
                    P.dma("sp", DBG["sc"], SC, reads=["sc"], semkey="dbgsc")
                P.barrier(NOPOOL)

            def ffn(src, wgu, wdn, k, tiles, dst_of, xl_out):
                kA, kB, kC = 3 * k, 3 * k + 1, 3 * k + 2
                g4 = max(1, DC // 4)

                def load_x(X, s):
                    for q in range(0, DC, g4):
                        P.dma("sp", X[:, q:q + g4, :], src[s, q:q + g4].rearrange("f p t -> p f t"),
                              writes=[("x", f) for f in range(q, q + g4)], semkey=("xld", q // g4))
                for ti_, (s, segs) in enumerate(tiles):
                    R1.reset()
                    R2.reset()
                    X = v3(R2.f32(DC * 512), DC)
                    XLR = [R2.bf(512) for _ in range(4)]
                    R2.reset()
                    HT = v3(R2.bf(FC * 512), FC)
                    XM = v3(R1.bf(DC * 512), DC)
                    R1.reset()
                    Y = v3(R1.f32(DC * 512), DC)
                    if ti_ == 0:
                        load_x(X, s)
                    stats_rstd([(X[:, f, :], ("x", f)) for f in range(DC)], RS[0], "rs0", D)
                    for f in range(DC):
                        mod_chunk(X[:, f, :], ("x", f), RS[0], "rs0", kA, kB, f, segs, XM[:, f, :], ("xm", f), f % 3)
                    if dbg and k == 0 and s == 0:
                        P.dma("sp", DBG["xm"], XM.rearrange("p a b -> p (a b)"), reads=[("xm", f) for f in range(DC)], semkey="dbgxm")
                        P.dma("sp", DBG["rs"], RS[0], reads=["rs0"], semkey="dbgrs")
                    P.barrier(NOPOOL)
                    for j in range(FC):
                        b0 = 2 * (j % 3)
                        for wi, bk in ((2 * j, b0), (2 * j + 1, b0 + 1)):
                            slot, sres = ws.get(wgu[wi], DC * 128)

                            def mm(e, slot=slot, bk=bk):
                                for kc in range(DC):
                                    ins = e.matmul(psb[bk][:], slot[:, kc * 128:(kc + 1) * 128], XM[:, kc, :],
                                                   start=(kc == 0), stop=(kc == DC - 1))
                                return ins
                            P.op("pe", mm, reads=[sres] + [("xm", f) for f in range(DC)], writes=[("ps", bk)])
                        ti = j % 3
                        P.op("act", lambda e, b0=b0, ti=ti: e.activation(out=TMP[ti], in_=psb[b0][:], func=AF.Silu),
                             reads=[("ps", b0)], writes=[("tmp", ti)])
                        P.op("dve", lambda e, b0=b0, ti=ti, j=j: e.tensor_tensor(out=HT[:, j, :], in0=TMP[ti], in1=psb[b0 + 1][:], op=ALU.mult),
                             reads=[("tmp", ti), ("ps", b0 + 1)], writes=[("ht", j)])
                    if dbg and k == 0 and s == 0:
                        P.dma("sp", DBG["ht"], HT.rearrange("p a b -> p (a b)"), reads=[("ht", f) for f in range(FC)], semkey="dbght")
                    P.barrier(NOPOOL)
                    for f in range(DC):
                        bk = nbank()
                        pieces = []
                        q = 0
                        while q < FC:
                            n = min(32, FC - q)
                            pieces.append((q, n))
                            q += n
                        for (q, n) in pieces:
                            slot, sres = ws.get(wdn[f][:, q * 128:(q + n) * 128], n * 128)

                            def mm(e, slot=slot, q=q, n=n, bk=bk):
                                for kk in range(n):
                                    ins = e.matmul(psb[bk][:], slot[:, kk * 128:(kk + 1) * 128], HT[:, q + kk, :],
                                                   start=(q + kk == 0), stop=(q + kk == FC - 1))
                                return ins
                            P.op("pe", mm, reads=[sres] + [("ht", q + kk) for kk in range(n)], writes=[("ps", bk)])
                        P.op("act", lambda e, bk=bk, f=f: e.activation(out=Y[:, f, :], in_=psb[bk][:], func=AF.Identity),
                             reads=[("ps", bk)], writes=[("y", f)])
                        stats_add(Y[:, f, :], ("y", f), f, f == 0, "dve")
                    if dbg and k == 0 and s == 0:
                        P.dma("sp", DBG["y"], Y.rearrange("p a b -> p (a b)"), reads=[("y", f) for f in range(DC)], semkey="dbgy")
                        P.barrier(NOPOOL)
                    if ti_ + 1 < len(tiles):
                        P.barrier(("sp",))
                        load_x(X, tiles[ti_ + 1][0])
                    stats_finish(RS[1], "rs1", D)

                    def xld(f):
                        P.dma("sp", TMP[f % 3], src[s, f], writes=[("tmp", f % 3)], semkey=("tmpld", f % 3))
                    xld(0)
                    xld(1)
                    for f in range(DC):
                        ti = f % 3
                        if f + 2 < DC:
                            xld(f + 2)
                        P.op("dve", lambda e, f=f: e.tensor_tensor(out=Y[:, f, :], in0=Y[:, f, :], in1=RS[1], op=ALU.mult),
                             reads=[("y", f), "rs1"], writes=[("y", f)])
                        for (c0, c1, var) in segs:
                            P.op("dve", lambda e, f=f, c0=c0, c1=c1, var=var, ti=ti: e.scalar_tensor_tensor(
                                out=Y[:, f, c0:c1], in0=Y[:, f, c0:c1], scalar=sc(kC, var, f), in1=TMP[ti][:, c0:c1], op0=ALU.mult, op1=ALU.add),
                                reads=[("y", f), ("tmp", ti), "sc"], writes=[("y", f)])
                        if xl_out:
                            stats_add(Y[:, f, :], ("y", f), f, f == 0, "pool")
                        d = dst_of(s)
                        if d is not None:
                            P.dma("sp", d[f], Y[:, f, :], reads=[("y", f)], semkey=("yst", f % 4))
                    if xl_out:
                        stats_finish(RS[0], "rs0", D)
                        P.barrier(NOPOOL)
                        for f in range(DC):
                            mod_chunk(Y[:, f, :], ("y", f), RS[0], "rs0", 3, 4, f, segs, XLR[f % 4], ("xlr", f % 4), f % 3)
                            P.dma("sp", S["xl"][s, f], XLR[f % 4], reads=[("xlr", f % 4)], semkey=("xlst", f % 4))
                    P.barrier(NOPOOL)

            def inproj():
                for s in range(NT):
                    extra = (s == NS)
                    foreign = (s > NS)
                    RM.reset()
                    XL = v3(RM.bf(DC * 512), DC)
                    COSR, SINR = RM.f32(512), RM.f32(512)
                    COSA, SINA = RM.f32(512), RM.f32(512)
                    CRT = RM.f32(4 * 256)
                    SRT = RM.f32(4 * 256)
                    XS = [RM.f32(512) for _ in range(4)]
                    T4 = [RM.f32(512) for _ in range(4)]
                    OB = [RM.bf(512) for _ in range(4)]
                    OF = [RM.f32(512) for _ in range(4)]
                    g4 = max(1, DC // 4)
                    for q in range(0, DC, g4):
                        P.dma("sp", XL[:, q:q + g4, :], S["xl"][s, q:q + g4].rearrange("f p t -> p f t"),
                              writes=[("xl", f) for f in range(q, q + g4)], semkey=("xld", q // g4))
                    if not extra and not foreign:
                        P.dma("sp", COSR, I["cosR"][s], writes=["cosr"], semkey="cosr")
                        P.dma("sp", SINR, I["sinR"][s], writes=["sinr"], semkey="sinr")
                    if not foreign:
                        P.dma("sp", COSA, I["cosA"][s], writes=["cosa"], semkey="cosa")
                        P.dma("sp", SINA, I["sinA"][s], writes=["sina"], semkey="sina")
                    P.dma("sp", v3(CRT, 4), I["cosRT"][s].rearrange("b p c -> p b c"), writes=["crt"], semkey="crt")
                    P.dma("sp", v3(SRT, 4), I["sinRT"][s].rearrange("b p c -> p b c"), writes=["srt"], semkey="srt")
                    ob_rr = [0]
                    of_rr = [0]
                    xs_rr = [0]

                    def fm_mm(ci):
                        slot, sres = ws.get(I["winfm"][ci], DC * 128)
                        bk = nbank()

                        def mm(e, slot=slot, bk=bk):
                            for kc in range(DC):
                                ins = e.matmul(psb[bk][:], slot[:, kc * 128:(kc + 1) * 128], XL[:, kc, :],
                                               start=(kc == 0), stop=(kc == DC - 1))
                            return ins
                        P.op("pe", mm, reads=[sres] + [("xl", f) for f in range(DC)], writes=[("ps", bk)])
                        return bk

                    def store_bf(src_i, dst):
                        P.dma("sp", dst, OB[src_i], reads=[("ob", src_i)], semkey=("obst", src_i))

                    ci = 0
                    while ci < len(fmcl):
                        kind, idx, _ = fmcl[ci]
                        gsz = 2 if kind in ("rq", "rk", "aq", "ak") else 1
                        if foreign or (extra and kind != "ak"):
                            ci += gsz
                            continue
                        if kind in ("rq", "rk"):
                            h = idx[0]
                            ba = fm_mm(ci)
                            bb = fm_mm(ci + 1)
                            ci += 2
                            scl = 1.0 if kind == "rq" else 0.0625
                            x1, x2 = xs_rr[0] % 4, (xs_rr[0] + 1) % 4
                            xs_rr[0] += 2
                            P.op("act", lambda e, ba=ba, x1=x1: e.activation(out=XS[x1], in_=psb[ba][:], func=AF.Identity),
                                 reads=[("ps", ba)], writes=[("xs", x1)])
                            P.op("act", lambda e, bb=bb, x2=x2: e.activation(out=XS[x2], in_=psb[bb][:], func=AF.Identity),
                                 reads=[("ps", bb)], writes=[("xs", x2)])
                            for half in range(2):
                                a, b_ = (x1, x2) if half == 0 else (x2, x1)
                                t0, t1 = 2 * half, 2 * half + 1
                                P.op("dve", lambda e, a=a, t0=t0, scl=scl: e.scalar_tensor_tensor(
                                    out=T4[t0], in0=XS[a], scalar=scl, in1=COSR, op0=ALU.mult, op1=ALU.mult),
                                    reads=[("xs", a), "cosr"], writes=[("t4", t0)])
                                P.op("dve", lambda e, b_=b_, t1=t1, scl=scl: e.scalar_tensor_tensor(
                                    out=T4[t1], in0=XS[b_], scalar=scl, in1=SINR, op0=ALU.mult, op1=ALU.mult),
                                    reads=[("xs", b_), "sinr"], writes=[("t4", t1)])
                                o = ob_rr[0] % 4
                                ob_rr[0] += 1
                                P.op("dve", lambda e, t0=t0, t1=t1, o=o, half=half: e.tensor_tensor(
                                    out=OB[o], in0=T4[t0], in1=T4[t1], op=(ALU.subtract if half == 0 else ALU.add)),
                                    reads=[("t4", t0), ("t4", t1)], writes=[("ob", o)])
                                store_bf(o, S[kind][h, half, :, s * 512:(s + 1) * 512])
                        elif kind in ("aq", "ak"):
                            ba = fm_mm(ci)
                            bb = fm_mm(ci + 1)
                            ci += 2
                            x1, x2 = xs_rr[0] % 4, (xs_rr[0] + 1) % 4
                            xs_rr[0] += 2
                            P.op("act", lambda e, ba=ba, x1=x1: e.activation(out=XS[x1], in_=psb[ba][:], func=AF.Identity),
                                 reads=[("ps", ba)], writes=[("xs", x1)])
                            P.op("act", lambda e, bb=bb, x2=x2: e.activation(out=XS[x2], in_=psb[bb][:], func=AF.Identity),
                                 reads=[("ps", bb)], writes=[("xs", x2)])
                            P.op("dve", lambda e, x1=x1: e.tensor_tensor(out=T4[0], in0=XS[x1], in1=COSA, op=ALU.mult),
                                 reads=[("xs", x1), "cosa"], writes=[("t4", 0)])
                            P.op("dve", lambda e, x2=x2: e.tensor_tensor(out=T4[1], in0=XS[x2], in1=SINA, op=ALU.mult),
                                 reads=[("xs", x2), "sina"], writes=[("t4", 1)])
                            o = ob_rr[0] % 4
                            ob_rr[0] += 1
                            P.op("dve", lambda e, o=o: e.tensor_tensor(out=OB[o], in0=T4[0], in1=T4[1], op=ALU.add),
                                 reads=[("t4", 0), ("t4", 1)], writes=[("ob", o)])
                            if kind == "aq":
                                store_bf(o, S["aq"][idx, :, s * 512:(s + 1) * 512])
                            elif not extra:
                                store_bf(o, S["ak"][idx, :, 128 + s * 512:128 + (s + 1) * 512])
                            else:
                                P.dma("sp", S["ak"][idx, :, 0:128], OB[o][:, 0:128], reads=[("ob", o)], semkey=("obst", o))
                                P.dma("sp", S["ak"][idx, :, T + 128:T + 256], OB[o][:, 128:256], reads=[("ob", o)], semkey=("obst", o))
                                P.dma("sp", S["cak"][idx], OB[o][:, 256:512], reads=[("ob", o)], semkey=("obst", o))
                        else:
                            bk = fm_mm(ci)
                            ci += 1
                            o = of_rr[0] % 4
                            of_rr[0] += 1
                            fn = AF.Silu if kind == "rg" else AF.Sigmoid
                            P.op("act", lambda e, bk=bk, o=o, fn=fn: e.activation(out=OF[o], in_=psb[bk][:], func=fn),
                                 reads=[("ps", bk)], writes=[("of", o)])
                            P.dma("sp", S[kind][idx, :, s * 512:(s + 1) * 512], OF[o], reads=[("of", o)], semkey=("ofst", o))
                    for bi, (kind, idx, c0, w) in enumerate(tmbl):
                        if foreign and kind == "av":
                            continue
                        kcg = tm_kcg[bi]
                        npc = DC // kcg
                        for pc in range(npc):
                            slot, sres = ws.get(I["wintm%d" % bi][pc], kcg * w)

                            def mm(e, slot=slot, pc=pc, kcg=kcg, w=w, npc=npc):
                                for tb in range(4):
                                    for kk in range(kcg):
                                        kc = pc * kcg + kk
                                        ins = e.matmul(psb[tb][:, 0:w], XL[:, kc, tb * 128:(tb + 1) * 128], slot[:, kk * w:(kk + 1) * w],
                                                       start=(kc == 0), stop=(kc == DC - 1))
                                return ins
                            P.op("pe", mm, reads=[sres] + [("xl", f) for f in range(DC)], writes=[("ps", tb) for tb in range(4)])
                        for tb in range(4):
                            o = ob_rr[0] % 4
                            ob_rr[0] += 1
                            if kind == "rk":
                                xi = xs_rr[0] % 4
                                xs_rr[0] += 1
                                P.op("act", lambda e, tb=tb, xi=xi, w=w: e.activation(out=XS[xi][:, 0:w], in_=psb[tb][:, 0:w], func=AF.Identity),
                                     reads=[("ps", tb)], writes=[("xs", xi)])
                                g = w // 256
                                xv = XS[xi][:, 0:w].rearrange("p (g h d) -> p g h d", g=g, h=2)
                                ov = OB[o][:, 0:w].rearrange("p (g h d) -> p g h d", g=g, h=2)
                                cv = CRT[:, tb * 256:tb * 256 + g * 128].rearrange("p (g d) -> p g d", g=g)
                                sv = SRT[:, tb * 256:tb * 256 + g * 128].rearrange("p (g d) -> p g d", g=g)
                                tv = [T4[i][:, 0:g * 128].rearrange("p (g d) -> p g d", g=g) for i in range(4)]
                                for half in range(2):
                                    a, b_ = (0, 1) if half == 0 else (1, 0)
                                    t0, t1 = 2 * half, 2 * half + 1
                                    P.op("dve", lambda e, a=a, t0=t0, xv=xv, cv=cv, tv=tv: e.scalar_tensor_tensor(
                                        out=tv[t0], in0=xv[:, :, a, :], scalar=0.0625, in1=cv, op0=ALU.mult, op1=ALU.mult),
                                        reads=[("xs", xi), "crt"], writes=[("t4", t0)])
                                    P.op("dve", lambda e, b_=b_, t1=t1, xv=xv, sv=sv, tv=tv: e.scalar_tensor_tensor(
                                        out=tv[t1], in0=xv[:, :, b_, :], scalar=0.0625, in1=sv, op0=ALU.mult, op1=ALU.mult),
                                        reads=[("xs", xi), "srt"], writes=[("t4", t1)])
                                    P.op("dve", lambda e, t0=t0, t1=t1, half=half, ov=ov, tv=tv: e.tensor_tensor(
                                        out=ov[:, :, half, :], in0=tv[t0], in1=tv[t1], op=(ALU.subtract if half == 0 else ALU.add)),
                                        reads=[("t4", t0), ("t4", t1)], writes=[("ob", o)])
                            else:
                                P.op("act", lambda e, tb=tb, o=o, w=w: e.activation(out=OB[o][:, 0:w], in_=psb[tb][:, 0:w], func=AF.Identity),
                                     reads=[("ps", tb)], writes=[("ob", o)])
                            if kind in ("rk", "rv"):
                                blk = (s * 4 + tb) if s < NS else (NB + (s - NS) * 4 + tb)
                                dst = S[kind + "tm"][blk][:, idx * w:(idx + 1) * w]
                            else:
                                if not extra:
                                    dst = S["av"][1 + s * 4 + tb][:, idx * w:(idx + 1) * w]
                                elif tb == 0:
                                    dst = S["av"][0][:, idx * w:(idx + 1) * w]
                                elif tb == 1:
                                    dst = S["av"][NB + 1][:, idx * w:(idx + 1) * w]
                                else:
                                    dst = S["cav"][tb - 2][:, idx * w:(idx + 1) * w]
                            P.dma("sp", dst, OB[o][:, 0:w], reads=[("ob", o)], semkey=("obst", o))
                    P.barrier(NOPOOL)

            def retention():
                NE = NFB + 1
                for h in range(RH):
                    RM.reset()
                    SR = [RM.f32(512), RM.f32(512)]
                    mark = RM.o
                    lgf, lgb = LG[:, h:h + 1], LG[:, RH + h:RH + h + 1]
                    P.op("act", lambda e, lgf=lgf: e.activation(out=HS[:, 0:1], in_=KCOL[:, 0:1], func=AF.Exp, scale=lgf), reads=["lg", ("c", "kcol")], writes=["hs"])
                    P.op("act", lambda e, lgb=lgb: e.activation(out=HS[:, 1:2], in_=KCOL[:, 1:2], func=AF.Exp, scale=lgb), reads=["lg", ("c", "kcol")], writes=["hs"])
                    P.op("act", lambda e, lgf=lgf: e.activation(out=HS[:, 2:3], in_=KCOL[:, 2:3], func=AF.Exp, scale=lgf), reads=["lg", ("c", "kcol")], writes=["hs"])
                    P.op("act", lambda e, lgb=lgb: e.activation(out=HS[:, 3:4], in_=KCOL[:, 2:3], func=AF.Exp, scale=lgb), reads=["lg", ("c", "kcol")], writes=["hs"])
                    for d in range(2):
                        lg = lgf if d == 0 else lgb
                        P.op("act", lambda e, d=d, lg=lg: e.activation(out=CO[:, d * NE:(d + 1) * NE], in_=ETAB[:, d * NE:(d + 1) * NE], func=AF.Exp, scale=lg),
                             reads=["lg", ("c", "etab")], writes=["co"])
                        P.op("dve", lambda e, d=d: e.tensor_tensor(out=CO[:, d * NE:(d + 1) * NE], in0=CO[:, d * NE:(d + 1) * NE],
                                                                  in1=ETAB[:, 2 * NE + d * NE:2 * NE + (d + 1) * NE], op=ALU.mult),
                             reads=["co", ("c", "etab")], writes=["co"])

                    def umm(kd, kres, vv, vres, c):
                        bk = nbank()

                        def mm(e, c=c, bk=bk):
                            for e2 in range(2):
                                ins = e.matmul(psb[bk][:, e2 * 256:(e2 + 1) * 256], kd[:, c, e2 * 128:(e2 + 1) * 128], vv[:, c, :],
                                               start=True, stop=True)
                            return ins
                        P.op("pe", mm, reads=[kres, vres], writes=[("ps", bk)])
                        return bk

                    def recur(kd, kres, vv, vres, order, d, have, snap=None):
                        Sd = SR[d]
                        g = HS[:, 2 + d:3 + d]
                        for c in order:
                            if snap is not None:
                                snap(c, Sd)
                            bk = umm(kd, kres, vv, vres, c)
                            if not have:
                                P.op("dve", lambda e, bk=bk: e.tensor_copy(out=Sd, in_=psb[bk][:]), reads=[("ps", bk)], writes=[("sr", d)])
                                have = True
                            else:
                                P.op("dve", lambda e, bk=bk: e.scalar_tensor_tensor(out=Sd, in0=Sd, scalar=g, in1=psb[bk][:], op0=ALU.mult, op1=ALU.add),
                                     reads=[("ps", bk), ("sr", d), "hs"], writes=[("sr", d)])

                    KF = v3(RM.bf(NFB * 256), NFB)
                    VF = v3(RM.bf(NFB * 256), NFB)
                    KFD = v3(RM.bf(NFB * 256), NFB)
                    CK = v3(RM.bf(512), 2)
                    CV = v3(RM.bf(512), 2)
                    CKF = v3(RM.bf(512), 2)
                    CKB = v3(RM.bf(512), 2)
                    hs_ = slice(h * 256, (h + 1) * 256)
                    P.dma("sp", KF, S["rktm"][NB + 4:NB + 4 + NFB, :, hs_].rearrange("b p c -> p b c"), writes=["kf"], semkey="kf")
                    P.dma("sp", VF, S["rvtm"][NB + 4:NB + 4 + NFB, :, hs_].rearrange("b p c -> p b c"), writes=["vf"], semkey="vf")
                    P.dma("sp", CK, S["rktm"][NB + 2:NB + 4, :, hs_].rearrange("b p c -> p b c"), writes=["ck"], semkey="ck")
                    P.dma("sp", CV, S["rvtm"][NB + 2:NB + 4, :, hs_].rearrange("b p c -> p b c"), writes=["cv"], semkey="cv")
                    P.op("dve", lambda e: e.tensor_scalar(out=KFD, in0=KF, scalar1=HS[:, 0:1], scalar2=None, op0=ALU.mult), reads=["kf", "hs"], writes=["kfd"])
                    P.op("pool", lambda e: e.tensor_scalar(out=KF, in0=KF, scalar1=HS[:, 1:2], scalar2=None, op0=ALU.mult), reads=["kf", "hs", "kfd"], writes=["kf"])
                    P.op("dve", lambda e: e.tensor_scalar(out=CKF, in0=CK, scalar1=HS[:, 0:1], scalar2=None, op0=ALU.mult), reads=["ck", "hs"], writes=["ckf"])
                    P.op("dve", lambda e: e.tensor_scalar(out=CKB, in0=CK, scalar1=HS[:, 1:2], scalar2=None, op0=ALU.mult), reads=["ck", "hs"], writes=["ckb"])
                    recur(CKF, "ckf", CV, "cv", [0, 1], 0, False)
                    recur(CKB, "ckb", CV, "cv", [1, 0], 1, False)
                    for d in range(2):
                        cc = CO[:, d * NE + NFB:d * NE + NFB + 1]
                        P.op("dve", lambda e, d=d, cc=cc: e.tensor_scalar(out=SR[d], in0=SR[d], scalar1=cc, scalar2=None, op0=ALU.mult),
                             reads=[("sr", d), "co"], writes=[("sr", d)])
                    for c in range(NFB):
                        for d in range(2):
                            bk = umm(KFD if d == 0 else KF, "kfd" if d == 0 else "kf", VF, "vf", c)
                            cc = CO[:, d * NE + c:d * NE + c + 1]
                            P.op("dve", lambda e, d=d, cc=cc, bk=bk: e.scalar_tensor_tensor(out=SR[d], in0=psb[bk][:], scalar=cc, in1=SR[d], op0=ALU.mult, op1=ALU.add),
                                 reads=[("ps", bk), ("sr", d), "co"], writes=[("sr", d)])
                    P.barrier()

                    RM.o = mark
                    KTM = v3(RM.bf(NB * 256), NB)
                    VTM = v3(RM.bf(NB * 256), NB)
                    KDF = v3(RM.bf(NB * 256), NB)
                    KDB = v3(RM.bf(NB * 256), NB)
                    P.dma("sp", KTM, S["rktm"][0:NB, :, hs_].rearrange("b p c -> p b c"), writes=["ktm"], semkey="ktm")
                    P.dma("sp", VTM, S["rvtm"][0:NB, :, hs_].rearrange("b p c -> p b c"), writes=["vtm"], semkey="vtm")
                    P.op("dve", lambda e: e.tensor_scalar(out=KDF, in0=KTM, scalar1=HS[:, 0:1], scalar2=None, op0=ALU.mult), reads=["ktm", "hs"], writes=["kdf"])
                    P.op("pool", lambda e: e.tensor_scalar(out=KDB, in0=KTM, scalar1=HS[:, 1:2], scalar2=None, op0=ALU.mult), reads=["ktm", "hs"], writes=["kdb"])
                    QT = v3(RM.bf(2 * T), 2)
                    KT = v3(RM.bf(2 * T), 2)
                    QF = v3(RM.bf(2 * T), 2)
                    QB = v3(RM.bf(2 * T), 2)
                    RG = v3(RM.f32(2 * T), 2)
                    OT = RM.f32(2 * T)
                    OT3 = v3(OT, 2)
                    SFB = [v3(RM.bf(512), 2) for _ in range(NB)]
                    SBB = [v3(RM.bf(512), 2) for _ in range(NB)]
                    DTR = RM.f32(512)
                    DT = RM.f32(128)
                    ATT = [RM.bf(128) for _ in range(3)]
                    OBR = [RM.bf(512) for _ in range(3)]
                    P.dma("sp", QT, S["rq"][h].rearrange("e p t -> p e t"), writes=["qt"], semkey="qt")
                    P.dma("sp", KT, S["rk"][h].rearrange("e p t -> p e t"), writes=["kt"], semkey="kt")
                    P.dma("sp", RG, S["rg"][2 * h:2 * h + 2].rearrange("e p t -> p e t"), writes=["rg"], semkey="rg")
                    P.dma("sp", OT, I["qidx"], writes=["ot"], semkey="otld")
                    P.dma("sp", DTR, I["dtab"], writes=["dtr"], semkey="dtr")

                    def snapf(c, Sd):
                        P.op("act", lambda e, c=c: e.activation(out=SFB[c].rearrange("p a b -> p (a b)"), in_=Sd, func=AF.Identity),
                             reads=[("sr", 0)], writes=[("sfb", c)])

                    def snapb(c, Sd):
                        P.op("act", lambda e, c=c: e.activation(out=SBB[c].rearrange("p a b -> p (a b)"), in_=Sd, func=AF.Identity),
                             reads=[("sr", 1)], writes=[("sbb", c)])
                    recur(KDF, "kdf", VTM, "vtm", list(range(NB)), 0, True, snapf)
                    recur(KDB, "kdb", VTM, "vtm", list(range(NB - 1, -1, -1)), 1, True, snapb)
                    P.op("act", lambda e, lgf=lgf: e.activation(out=OT[:, 0:T], in_=OT[:, 0:T], func=AF.Exp, scale=lgf), reads=["ot", "lg"], writes=["ot"])
                    P.op("act", lambda e, lgb=lgb: e.activation(out=OT[:, T:2 * T], in_=OT[:, T:2 * T], func=AF.Exp, scale=lgb), reads=["ot", "lg"], writes=["ot"])
                    for e2 in range(2):
                        P.op("dve", lambda e, e2=e2: e.tensor_tensor(out=QF[:, e2, :], in0=QT[:, e2, :], in1=OT[:, 0:T], op=ALU.mult),
                             reads=["qt", "ot"], writes=["qf"])
                        P.op("pool", lambda e, e2=e2: e.tensor_tensor(out=QB[:, e2, :], in0=QT[:, e2, :], in1=OT[:, T:2 * T], op=ALU.mult),
                             reads=["qt", "ot"], writes=["qb"])
                    P.op("act", lambda e, lgf=lgf: e.activation(out=DTR[:, 0:128], in_=DTR[:, 0:128], func=AF.Exp, scale=lgf), reads=["dtr", "lg"], writes=["dtr"])
                    P.op("act", lambda e, lgb=lgb: e.activation(out=DTR[:, 256:384], in_=DTR[:, 256:384], func=AF.Exp, scale=lgb), reads=["dtr", "lg"], writes=["dtr"])
                    P.op("dve", lambda e: e.tensor_tensor(out=DTR[:, 0:128], in0=DTR[:, 0:128], in1=DTR[:, 128:256], op=ALU.mult), reads=["dtr"], writes=["dtr"])
                    P.op("dve", lambda e: e.tensor_tensor(out=DTR[:, 256:384], in0=DTR[:, 256:384], in1=DTR[:, 384:512], op=ALU.mult), reads=["dtr"], writes=["dtr"])
                    P.op("dve", lambda e: e.tensor_tensor(out=DT, in0=DTR[:, 0:128], in1=DTR[:, 256:384], op=ALU.add), reads=["dtr"], writes=["dt"])
                    P.barrier()
                    for c in range(NB):
                        cs = slice(c * 128, (c + 1) * 128)
                        br = nbank()

                        def mmr(e, br=br, cs=cs):
                            for e2 in range(2):
                                ins = e.matmul(psb[br][:, 0:128], KT[:, e2, cs], QT[:, e2, cs], start=(e2 == 0), stop=(e2 == 1))
                            return ins
                        P.op("pe", mmr, reads=["kt", "qt"], writes=[("ps", br)])
                        ai = c % 3
                        P.op("dve", lambda e, br=br, ai=ai: e.tensor_tensor(out=ATT[ai], in0=psb[br][:, 0:128], in1=DT, op=ALU.mult),
                             reads=[("ps", br), "dt"], writes=[("att", ai)])
                        bo = nbank()

                        def mmo(e, bo=bo, c=c, cs=cs, ai=ai):
                            for e2 in range(2):
                                o = psb[bo][:, e2 * 128:(e2 + 1) * 128]
                                ds_ = slice(e2 * 128, (e2 + 1) * 128)
                                e.matmul(o, VTM[:, c, ds_], ATT[ai], start=True, stop=False)
                                for ee in range(2):
                                    e.matmul(o, SFB[c][:, ee, ds_], QF[:, ee, cs], start=False, stop=False)
                                for ee in range(2):
                                    ins = e.matmul(o, SBB[c][:, ee, ds_], QB[:, ee, cs], start=False, stop=(ee == 1))
                            return ins
                        P.op("pe", mmo, reads=["vtm", ("att", ai), ("sfb", c), ("sbb", c), "qf", "qb"], writes=[("ps", bo)])
                        P.op("act", lambda e, bo=bo, cs=cs: e.activation(out=OT3[:, :, cs], in_=psb[bo][:, 0:256].rearrange("p (a b) -> p a b", a=2), func=AF.Identity),
                             reads=[("ps", bo)], writes=[("otc", c)])
                    for q in range(NS):
                        ts_ = slice(q * 512, (q + 1) * 512)
                        rres = [("otc", c) for c in range(q * 4, q * 4 + 4)]
                        stats_rstd([(OT3[:, 0, ts_], rres), (OT3[:, 1, ts_], rres)], RS[0], "rs0", 256.0)
                        for e2 in range(2):
                            ti = e2
                            P.op("dve", lambda e, e2=e2, ts_=ts_, ti=ti: e.tensor_tensor(out=TMP[ti], in0=OT3[:, e2, ts_], in1=RS[0], op=ALU.mult),
                                 reads=rres + ["rs0"], writes=[("tmp", ti)])
                            oi = (q * 2 + e2) % 3
                            gcol = GNT[:, 2 * h + e2:2 * h + e2 + 1]
                            P.op("dve", lambda e, e2=e2, ts_=ts_, ti=ti, oi=oi, gcol=gcol: e.scalar_tensor_tensor(
                                out=OBR[oi], in0=TMP[ti], scalar=gcol, in1=RG[:, e2, ts_], op0=ALU.mult, op1=ALU.mult),
                                reads=[("tmp", ti), "rg", ("c", "gn")], writes=[("obr", oi)])
                            P.dma("sp", S["retl"][2 * h + e2, :, ts_], OBR[oi], reads=[("obr", oi)], semkey=("obrst", oi))
                    P.barrier()

            def attention():
                scale = 128.0 ** -0.5
                for g in range(KVH):
                    RM.reset()
                    AKE = RM.bf(T + 256)
                    CAK = RM.bf(256)
                    AVE = v3(RM.bf((NB + 2) * 128), NB + 2)
                    CAV = v3(RM.bf(256), 2)
                    MASK = RM.f32(3 * 384)
                    AQ = [RM.bf(T) for _ in range(2)]
                    AOT = [RM.bf(T) for _ in range(2)]
                    SS = [RM.f32(640) for _ in range(2)]
                    PB = [RM.bf(640) for _ in range(2)]
                    PT = [RM.bf(640) for _ in range(2)]
                    P.dma("sp", AKE, S["ak"][g], writes=["ake"], semkey="ake")
                    P.dma("sp", CAK, S["cak"][g], writes=["cak"], semkey="cak")
                    P.dma("sp", AVE, S["av"][:, :, g * 128:(g + 1) * 128].rearrange("b p c -> p b c"), writes=["ave"], semkey="ave")
                    P.dma("sp", CAV, S["cav"][:, :, g * 128:(g + 1) * 128].rearrange("b p c -> p b c"), writes=["cav"], semkey="cav")
                    P.dma("sp", MASK, I["amask"], writes=["mask"], semkey="mask")
                    for hq in range(4):
                        h = g * 4 + hq
                        qi = h % 2
                        P.dma("sp", AQ[qi], S["aq"][h], writes=[("aq", qi)], semkey=("aq", qi))
                        skc = SINK[:, h:h + 1]
                        for b in range(NB):
                            bs = slice(b * 128, (b + 1) * 128)
                            which = 0 if b == 0 else (2 if b == NB - 1 else 1)
                            si = b % 2
                            b1 = nbank()
                            b2 = nbank()

                            def mms(e, b1=b1, b2=b2, bs=bs, b=b, qi=qi):
                                e.matmul(psb[b1][:, 0:384], AQ[qi][:, bs], AKE[:, b * 128:b * 128 + 384], start=True, stop=True)
                                return e.matmul(psb[b2][:, 0:256], AQ[qi][:, bs], CAK, start=True, stop=True)
                            P.op("pe", mms, reads=[("aq", qi), "ake", "cak"], writes=[("ps", b1), ("ps", b2)])
                            P.op("dve", lambda e, b1=b1, si=si, which=which: e.scalar_tensor_tensor(
                                out=SS[si][:, 0:384], in0=psb[b1][:, 0:384], scalar=scale, in1=MASK[:, which * 384:(which + 1) * 384], op0=ALU.mult, op1=ALU.add),
                                reads=[("ps", b1), "mask"], writes=[("ss", si)])
                            P.op("act", lambda e, b2=b2, si=si: e.activation(out=SS[si][:, 384:640], in_=psb[b2][:, 0:256], func=AF.Identity, scale=scale),
                                 reads=[("ps", b2)], writes=[("ss", si)])
                            c0 = (b % 2) * 8
                            mx, nmx, sm, es, den = (COLS[:, c0 + i:c0 + i + 1] for i in range(5))
                            cres = ("cols", b % 2)
                            P.op("dve", lambda e, si=si, mx=mx: e.reduce_max(out=mx, in_=SS[si], axis=mybir.AxisListType.X), reads=[("ss", si)], writes=[cres])
                            P.op("dve", lambda e, mx=mx, skc=skc: e.tensor_tensor(out=mx, in0=mx, in1=skc, op=ALU.max), reads=[cres, ("c", "sink")], writes=[cres])
                            P.op("dve", lambda e, mx=mx, nmx=nmx: e.tensor_scalar(out=nmx, in0=mx, scalar1=-1.0, scalar2=None, op0=ALU.mult), reads=[cres], writes=[cres])
                            P.op("dve", lambda e, sm=sm: e.memset(sm, 0.0), writes=[cres])
                            P.op("act", lambda e, si=si, nmx=nmx, sm=sm: e.activation(out=SS[si], in_=SS[si], func=AF.Exp, bias=nmx, scale=1.0, accum_out=sm),
                                 reads=[("ss", si), cres], writes=[("ss", si), cres])
                            P.op("act", lambda e, nmx=nmx, es=es, skc=skc: e.activation(out=es, in_=skc, func=AF.Exp, bias=nmx, scale=1.0),
                                 reads=[cres, ("c", "sink")], writes=[cres])
                            P.op("dve", lambda e, sm=sm, es=es, den=den: e.tensor_tensor(out=den, in0=sm, in1=es, op=ALU.add), reads=[cres], writes=[cres])
                            P.op("dve", lambda e, den=den: e.reciprocal(out=den, in_=den), reads=[cres], writes=[cres])
                            P.op("dve", lambda e, si=si, den=den: e.tensor_scalar(out=PB[si], in0=SS[si], scalar1=den, scalar2=None, op0=ALU.mult),
                                 reads=[("ss", si), cres], writes=[("pb", si)])

                            def mmt(e, si=si):
                                for kc in range(5):
                                    ins = e.transpose(pst[:, kc * 128:(kc + 1) * 128], PB[si][:, kc * 128:(kc + 1) * 128], IDENT)
                                return ins
                            P.op("pe", mmt, reads=[("pb", si), "ident"], writes=["pst"])
                            P.op("act", lambda e, si=si: e.activation(out=PT[si], in_=pst[:, 0:640], func=AF.Identity), reads=["pst"], writes=[("pt", si)])
                            b3 = nbank()

                            def mmo(e, b3=b3, b=b, si=si):
                                for kc in range(5):
                                    vv = AVE[:, b + kc, :] if kc < 3 else CAV[:, kc - 3, :]
                                    ins = e.matmul(psb[b3][:, 0:128], vv, PT[si][:, kc * 128:(kc + 1) * 128], start=(kc == 0), stop=(kc == 4))
                                return ins
                            P.op("pe", mmo, reads=[("pt", si), "ave", "cav"], writes=[("ps", b3)])
                            P.op("act", lambda e, b3=b3, bs=bs, qi=qi: e.activation(out=AOT[qi][:, bs], in_=psb[b3][:, 0:128], func=AF.Identity),
                                 reads=[("ps", b3)], writes=[("aot", qi)])
                        P.dma("sp", S["attl"][h], AOT[qi], reads=[("aot", qi)], semkey=("aotst", qi))
                    P.barrier()

            def merge():
                for s in range(NS):
                    ts_ = slice(s * 512, (s + 1) * 512)
                    R1.reset()
                    R2.reset()
                    Y = v3(R1.f32(DC * 512), DC)
                    RL = v3(R2.bf(2 * RH * 512), 2 * RH)
                    AL = v3(R2.bf(AH * 512), AH)
                    Z = v3(R2.bf(DC * 512), DC)
                    GA = [R2.f32(512) for _ in range(2)]
                    GB = [R2.f32(512) for _ in range(2)]
                    P.dma("sp", RL, S["retl"][:, :, ts_].rearrange("c p t -> p c t"), writes=["rl"], semkey="rl")
                    P.dma("sp", AL, S["attl"][:, :, ts_].rearrange("c p t -> p c t"), writes=["al"], semkey="al")
                    for f in range(DC):
                        gi = f % 2
                        P.dma("sp", GA[gi], S["ga"][f, :, ts_], writes=[("ga", gi)], semkey=("ga", gi))
                        P.dma("sp", GB[gi], S["gb"][f, :, ts_], writes=[("gb", gi)], semkey=("gb", gi))
                        ba, bb = nbank(), nbank()

                        s1, r1 = ws.get(I["wrup"][f], 2 * RH * 128)

                        def mma(e, s1=s1, ba=ba):
                            for kc in range(2 * RH):
                                ins = e.matmul(psb[ba][:], s1[:, kc * 128:(kc + 1) * 128], RL[:, kc, :], start=(kc == 0), stop=(kc == 2 * RH - 1))
                            return ins

                        P.op("pe", mma, reads=[r1, "rl"], writes=[("ps", ba)])
                        s2, r2 = ws.get(I["waup"][f], AH * 128)

                        def mmb(e, s2=s2, bb=bb):
                            for kc in range(AH):
                                ins = e.matmul(psb[bb][:], s2[:, kc * 128:(kc + 1) * 128], AL[:, kc, :], start=(kc == 0), stop=(kc == AH - 1))
                            return ins
                        P.op("pe", mmb, reads=[r2, "al"], writes=[("ps", bb)])
                        P.op("dve", lambda e, gi=gi, ba=ba: e.tensor_tensor(out=GA[gi], in0=GA[gi], in1=psb[ba][:], op=ALU.mult),
                             reads=[("ga", gi), ("ps", ba)], writes=[("ga", gi)])
                        P.op("dve", lambda e, gi=gi, bb=bb: e.tensor_tensor(out=GB[gi], in0=GB[gi], in1=psb[bb][:], op=ALU.mult),
                             reads=[("gb", gi), ("ps", bb)], writes=[("gb", gi)])
                        P.op("dve", lambda e, gi=gi, f=f: e.tensor_tensor(out=Z[:, f, :], in0=GA[gi], in1=GB[gi], op=ALU.add),
                             reads=[("ga", gi), ("gb", gi)], writes=[("z", f)])
                    for f in range(DC):
                        slot, sres = ws.get(I["wout"][f], DC * 128)
                        bk = nbank()

                        def mm(e, slot=slot, bk=bk):
                            for kc in range(DC):
                                ins = e.matmul(psb[bk][:], slot[:, kc * 128:(kc + 1) * 128], Z[:, kc, :], start=(kc == 0), stop=(kc == DC - 1))
                            return ins
                        P.op("pe", mm, reads=[sres] + [("z", k) for k in range(DC)], writes=[("ps", bk)])
                        P.op("act", lambda e, bk=bk, f=f: e.activation(out=Y[:, f, :], in_=psb[bk][:], func=AF.Identity),
                             reads=[("ps", bk)], writes=[("y", f)])
                        stats_add(Y[:, f, :], ("y", f), f, f == 0, "dve")
                    stats_finish(RS[1], "rs1", D)

                    def hld(f):
                        P.dma("sp", TMP[f % 3], S["h1"][s, f], writes=[("tmp", f % 3)], semkey=("tmpld", f % 3))
                    hld(0)
                    hld(1)
                    for f in range(DC):
                        ti = f % 3
                        if f + 2 < DC:
                            hld(f + 2)
                        P.op("dve", lambda e, f=f: e.tensor_tensor(out=Y[:, f, :], in0=Y[:, f, :], in1=RS[1], op=ALU.mult),
                             reads=[("y", f), "rs1"], writes=[("y", f)])
                        P.op("dve", lambda e, f=f, ti=ti: e.scalar_tensor_tensor(
                            out=Y[:, f, :], in0=Y[:, f, :], scalar=sc(5, 0, f), in1=TMP[ti], op0=ALU.mult, op1=ALU.add),
                            reads=[("y", f), ("tmp", ti), "sc"], writes=[("y", f)])
                        P.dma("sp", S["h2"][s, f], Y[:, f, :], reads=[("y", f)], semkey=("yst", f % 4))
                    P.barrier(NOPOOL)

            own = [(s, [(0, 512, 0)]) for s in range(NS)]
            ext = [(NS, [(0, 256, 0), (256, 512, 1)])]
            frn = [(NS + 1 + i, [(0, 512, 0)]) for i in range(NF)]
            phase0()
            if upto >= 1:
                ffn(I["xT"], I["wgu1"], I["wdn1"], 0, own + ext + frn, lambda s: (S["h1"][s] if s < NS else None), True)
                P.barrier()
            if upto >= 2:
                inproj()
                P.barrier()
            if upto >= 3:
                retention()
                P.barrier()
            if upto >= 4:
                attention()
                P.barrier()
            if upto >= 5:
                merge()
                P.barrier()
            if upto >= 6:
                ffn(S["h2"], I["wgu2"], I["wdn2"], 2, own, lambda s: outT[s], False)

        Pd = Prog(nc, dry=True)
        wsd = WStream(Pd, WR)
        run_pass(Pd, wsd)
        Pr = Prog(nc)
        wsr = WStream(Pr, WR, plan=wsd.req)
        run_pass(Pr, wsr)
        assert wsr.k == len(wsd.req)
        Pr.emit()
        build.stats = Pr.stats
    return nc


_CACHE = {}


def run_cfg(inputs, cfg, upto=99, dbg=False):
    inp = {k: np.asarray(v, dtype=np.float32) for k, v in inputs.items()}
    per_core = prep_host(inp, cfg)
    key = (cfg.D, cfg.DFF, cfg.SEQ, upto, dbg)
    if key not in _CACHE:
        _CACHE[key] = build(cfg, upto, dbg)
    nc = _CACHE[key]
    res = run_bass_kernel_spmd(nc, per_core, core_ids=list(range(8)))
    out = np.empty((cfg.B, cfg.SEQ, cfg.D), np.float32)
    for core in range(8):
        b, j = core // 4, core % 4
        o = res.results[core]["outT"]
        o = o.transpose(0, 3, 1, 2).reshape(cfg.T, cfg.D)
        out[b, j * cfg.T:(j + 1) * cfg.T] = o
    return out, res


def kernel(**inputs):
    out, _ = run_cfg(inputs, Cfg())
    return out
```

```python
import contextlib
import numpy as np
import concourse.bass as bass
import concourse.mybir as mybir
from concourse.bass_utils import run_bass_kernel_spmd

F32 = mybir.dt.float32
BF16 = mybir.dt.bfloat16
AF = mybir.ActivationFunctionType
ALU = mybir.AluOpType
ENGS = ("pe", "act", "dve", "pool", "sp")
EPS = 1e-6
NEG = -1e30


class Op:
    __slots__ = ("eng", "fn", "reads", "writes", "kind", "semkey", "deps",
                 "signal", "ticket", "idx", "is_barrier")

    def __init__(self, eng, fn, reads, writes, kind, semkey=None):
        self.eng = eng
        self.fn = fn
        self.reads = reads
        self.writes = writes
        self.kind = kind
        self.semkey = semkey
        self.deps = ()
        self.signal = False
        self.ticket = 0
        self.is_barrier = False


class Prog:
    def __init__(self, nc, dry=False):
        self.nc = nc
        self.dry = dry
        self.ops = []
        self.last_w = {}
        self.readers = {}
        self.last_on_eng = {e: None for e in ENGS}
        self.last_dma_on_key = {}

    def _rec(self, op):
        if self.dry:
            return op
        deps = set()
        lw = self.last_w
        rd = self.readers
        for r in op.reads:
            p = lw.get(r)
            if p is not None:
                deps.add(p)
            rd.setdefault(r, []).append(op)
        for w in op.writes:
            p = lw.get(w)
            if p is not None:
                deps.add(p)
            rl = rd.get(w)
            if rl:
                deps.update(rl)
            lw[w] = op
            rd[w] = []
        deps.discard(op)
        op.deps = deps
        op.idx = len(self.ops)
        self.ops.append(op)
        self.last_on_eng[op.eng] = op
        if op.kind == "d":
            self.last_dma_on_key[op.semkey] = op
        return op

    def op(self, eng, fn, reads=(), writes=()):
        return self._rec(Op(eng, fn, tuple(reads), tuple(writes), "c"))

    def dma(self, eng, out, in_, reads=(), writes=(), semkey=None):
        assert semkey is not None

        def fn(e, out=out, in_=in_):
            return e.dma_start(out=out, in_=in_)

        return self._rec(Op(eng, fn, tuple(reads), tuple(writes), "d", semkey))

    def custom_dma(self, eng, fn, reads=(), writes=(), semkey=None):
        return self._rec(Op(eng, fn, tuple(reads), tuple(writes), "d", semkey))

    def barrier(self, engs=ENGS):
        if self.dry:
            return
        pend = [o for o in self.last_on_eng.values() if o is not None and o.kind == "c"]
        pend += list(self.last_dma_on_key.values())
        for e in engs:
            o = Op(e, None, (), (), "c")
            o.is_barrier = True
            o.deps = set(pend)
            o.idx = len(self.ops)
            self.ops.append(o)

    def emit(self, final_wait_eng="sp"):
        nc = self.nc
        ops = self.ops
        for o in ops:
            for p in o.deps:
                if p.kind == "c" and p.eng == "pe" and o.eng == "pe" and not o.is_barrier:
                    continue
                p.signal = True
        finals = [o for o in self.last_on_eng.values() if o is not None and o.kind == "c"]
        finals += list(self.last_dma_on_key.values())
        for o in finals:
            o.signal = True
        for o in ops:
            if o.kind == "d":
                o.signal = True
        cnt = {e: 0 for e in ENGS}
        dcnt = {}
        for o in ops:
            if o.is_barrier or not o.signal:
                continue
            if o.kind == "c":
                cnt[o.eng] += 1
                o.ticket = cnt[o.eng]
            else:
                dcnt[o.semkey] = dcnt.get(o.semkey, 0) + 16
                o.ticket = dcnt[o.semkey]
        self.stats = (dict(cnt), len(dcnt), len(ops))
        with contextlib.ExitStack() as st:
            esem = {e: st.enter_context(nc.semaphore("s_" + e)) for e in ENGS if e != "sp"}
            dsem = {}
            for k in dcnt:
                dsem[k] = st.enter_context(nc.semaphore("d_%d" % len(dsem)))
            block = st.enter_context(nc.Block())
            per_eng = {e: [o for o in ops if o.eng == e] for e in ENGS}

            def run(e, eng):
                waited = {}
                for o in per_eng[e]:
                    for p in sorted(o.deps, key=lambda q: q.idx):
                        if p.kind == "c":
                            if p.eng == "pe" and e == "pe" and not o.is_barrier:
                                continue
                            sem = esem[p.eng]
                            key = ("e", p.eng)
                        else:
                            sem = dsem[p.semkey]
                            key = ("d", p.semkey)
                        if waited.get(key, 0) >= p.ticket:
                            continue
                        waited[key] = p.ticket
                        eng.wait_ge(sem, p.ticket)
                    if o.is_barrier:
                        continue
                    ins = o.fn(eng)
                    if o.signal:
                        if o.kind == "c":
                            ins.then_inc(esem[o.eng], 1)
                        else:
                            ins.then_inc(dsem[o.semkey], 16)
                if e == final_wait_eng:
                    for p in finals:
                        if p.kind == "c":
                            if waited.get(("e", p.eng), 0) < p.ticket:
                                eng.wait_ge(esem[p.eng], p.ticket)
                        else:
                            if waited.get(("d", p.semkey), 0) < p.ticket:
                                eng.wait_ge(dsem[p.semkey], p.ticket)

            @block.tensor
            def _(eng):
                run("pe", eng)

            @block.scalar
            def _(eng):
                run("act", eng)

            @block.vector
            def _(eng):
                run("dve", eng)

            @block.gpsimd
            def _(eng):
                run("pool", eng)

            @block.sync
            def _(eng):
                run("sp", eng)


class Cfg:
    def __init__(self, D=4096, DFF=11008, SEQ=8192, B=2, CTX=256, GRID_W=64):
        self.D, self.DFF, self.SEQ, self.B, self.CTX, self.GRID_W = D, DFF, SEQ, B, CTX, GRID_W
        self.DC = D // 128
        self.FC = DFF // 128
        self.M = D // 2
        self.RH = self.M // 256
        self.AH = self.M // 128
        self.KVH = self.AH // 4
        self.T = SEQ // 4
        self.NB = self.T // 128
        self.NS = self.T // 512
        self.NF = 3 * self.NS
        self.NT = self.NS + 1 + self.NF
        self.NFB = 3 * self.NB
        M = self.M
        self.o_rq, self.o_rk, self.o_rv, self.o_rg, self.o_aq = 0, M, 2 * M, 3 * M, 4 * M
        self.o_ak = 5 * M
        self.o_av = 5 * M + M // 4
        self.o_ga = 5 * M + M // 2
        self.o_gb = self.o_ga + D
        self.INC = self.o_gb + D
        self.WSLOT = 4096
        self.NWS = 4


def fm_chunk_list(cfg):
    out = []
    perm = np.concatenate([np.arange(32, 64), np.arange(0, 32), np.arange(96, 128), np.arange(64, 96)])
    for h in range(cfg.RH):
        for e in range(2):
            out.append(("rq", (h, e), cfg.o_rq + h * 256 + e * 128 + np.arange(128)))
    for h in range(cfg.RH):
        for e in range(2):
            out.append(("rk", (h, e), cfg.o_rk + h * 256 + e * 128 + np.arange(128)))
    for c in range(2 * cfg.RH):
        out.append(("rg", c, cfg.o_rg + c * 128 + np.arange(128)))
    for h in range(cfg.AH):
        out.append(("aq", h, cfg.o_aq + h * 128 + np.arange(128)))
        out.append(("aqs", h, cfg.o_aq + h * 128 + perm))
    for h in range(cfg.KVH):
        out.append(("ak", h, cfg.o_ak + h * 128 + np.arange(128)))
        out.append(("aks", h, cfg.o_ak + h * 128 + perm))
    for c in range(cfg.DC):
        out.append(("ga", c, cfg.o_ga + c * 128 + np.arange(128)))
    for c in range(cfg.DC):
        out.append(("gb", c, cfg.o_gb + c * 128 + np.arange(128)))
    return out


def tm_block_list(cfg):
    out = []
    W = min(512, cfg.RH * 256)
    for i in range(cfg.RH * 256 // W):
        out.append(("rk", i, cfg.o_rk + i * W, W))
    for i in range(cfg.RH * 256 // W):
        out.append(("rv", i, cfg.o_rv + i * W, W))
    Wv = min(512, cfg.KVH * 128)
    for i in range(cfg.KVH * 128 // Wv):
        out.append(("av", i, cfg.o_av + i * Wv, Wv))
    return out


def fm_tiles(W, col_lists):
    K = W.shape[0]
    KC = K // 128
    out = np.empty((len(col_lists), 128, KC * 128), np.float32)
    for i, cols in enumerate(col_lists):
        blk = W[:, cols].reshape(KC, 128, 128)
        out[i] = blk.transpose(1, 0, 2).reshape(128, KC * 128)
    return out


def tm_tiles(W, col0, width, wslot):
    K = W.shape[0]
    KC = K // 128
    kcg = min(wslot // width, KC)
    npc = KC // kcg
    blk = W[:, col0:col0 + width].reshape(npc, kcg, 128, width)
    return np.ascontiguousarray(blk.transpose(0, 2, 1, 3).reshape(npc, 128, kcg * width)), kcg


def prep_host(inp, cfg):
    D, DC, T, NS, NB, RH, AH, KVH = cfg.D, cfg.DC, cfg.T, cfg.NS, cfg.NB, cfg.RH, cfg.AH, cfg.KVH
    f32 = np.float32
    x, c, ctx, c_ctx = inp["x"], inp["c"], inp["ctx"], inp["c_ctx"]
    shared = {}
    shared["wada"] = fm_tiles(inp["w_ada"][0], [np.arange(i * 128, (i + 1) * 128) for i in range(9 * DC)])
    for nm, gu, dn in (("1", "ffn1_w_gu", "ffn1_w_dn"), ("2", "ffn2_w_gu", "ffn2_w_dn")):
        cl = []
        for j in range(cfg.FC):
            cl.append(np.arange(j * 128, (j + 1) * 128))
            cl.append(cfg.DFF + np.arange(j * 128, (j + 1) * 128))
        shared["wgu" + nm] = fm_tiles(inp[gu][0], cl)
        shared["wdn" + nm] = fm_tiles(inp[dn][0], [np.arange(i * 128, (i + 1) * 128) for i in range(DC)])
    win = inp["w_in"][0]
    shared["winfm"] = fm_tiles(win, [cl for _, _, cl in fm_chunk_list(cfg)])
    for i, (kind, idx, c0, w) in enumerate(tm_block_list(cfg)):
        shared["wintm%d" % i], _ = tm_tiles(win, c0, w, cfg.WSLOT)
    shared["wrup"] = fm_tiles(inp["w_ret_up"][0], [np.arange(i * 128, (i + 1) * 128) for i in range(DC)])
    shared["waup"] = fm_tiles(inp["w_att_up"][0], [np.arange(i * 128, (i + 1) * 128) for i in range(DC)])
    shared["wout"] = fm_tiles(inp["w_out"][0], [np.arange(i * 128, (i + 1) * 128) for i in range(DC)])

    def fm_vec(v):
        return np.ascontiguousarray(v.reshape(-1, 128).T)

    shared["bada"] = fm_vec(inp["b_ada"][0])
    shared["gpre"] = np.ascontiguousarray(inp["norm_pre"][0].reshape(3, DC, 128).transpose(2, 0, 1).reshape(128, 3 * DC))
    shared["gpost"] = np.ascontiguousarray(inp["norm_post"][0].reshape(3, DC, 128).transpose(2, 0, 1).reshape(128, 3 * DC))
    shared["rate"] = np.ascontiguousarray(np.broadcast_to(inp["ret_log_rate"][0].reshape(1, 2 * RH), (128, 2 * RH))).astype(f32)
    shared["gn"] = fm_vec(inp["ret_gn"][0])
    shared["sink"] = np.ascontiguousarray(np.broadcast_to(inp["attn_sink"][0].reshape(1, AH), (128, AH))).astype(f32)
    ii = np.arange(128, dtype=f32)
    diff = ii[None, :] - ii[:, None]
    shared["dtab"] = np.concatenate([np.maximum(diff, 0), (diff >= 0).astype(f32),
                                     np.maximum(-diff, 0), (diff <= 0).astype(f32)], axis=1).astype(f32)
    tl = np.arange(T) % 128
    shared["qidx"] = np.ascontiguousarray(np.broadcast_to(
        np.concatenate([tl + 1.0, 128.0 - tl]).astype(f32)[None, :], (128, 2 * T)))
    shared["ident"] = np.eye(128, dtype=f32)
    shared["kcol"] = np.stack([127.0 - ii, ii, np.full(128, 128.0, f32)], axis=1).astype(f32)

    theta = (1.0 / 10000.0 ** np.linspace(0.0, 1.0, 128, dtype=f32)).astype(f32)
    nf = 32
    inv = (10000.0 ** (-np.arange(nf, dtype=f32) / nf)).astype(f32)

    per_core = []
    for core in range(8):
        b, j = core // 4, core % 4
        d = dict(shared)
        t0 = j * T
        NF, NT, NFB = cfg.NF, cfg.NT, cfg.NFB
        xt = np.zeros((NT, 512, D), f32)
        xt[:NS] = x[b, t0:t0 + T].reshape(NS, 512, D)
        if j > 0:
            xt[NS, 0:128] = x[b, t0 - 128:t0]
        if j < 3:
            xt[NS, 128:256] = x[b, t0 + T:t0 + T + 128]
        xt[NS, 256:512] = ctx[b]
        fpos = np.concatenate([np.arange(sg * T, (sg + 1) * T) for sg in range(4) if sg != j])
        xt[NS + 1:] = x[b, fpos].reshape(NF, 512, D)
        d["xT"] = np.ascontiguousarray(xt.reshape(NT, 512, DC, 128).transpose(0, 2, 3, 1))
        d["cT"] = np.ascontiguousarray(np.stack([c[b], c_ctx], axis=1).reshape(DC, 128, 2).transpose(1, 0, 2).reshape(128, DC * 2))
        pos_own = (t0 + np.arange(T)).astype(f32)
        pos_ext = np.concatenate([t0 - 128 + np.arange(128), t0 + T + np.arange(128)]).astype(f32)
        ang = pos_own[None, :] * theta[:, None]
        d["cosR"] = np.ascontiguousarray(np.cos(ang).astype(f32).reshape(128, NS, 512).transpose(1, 0, 2))
        d["sinR"] = np.ascontiguousarray(np.sin(ang).astype(f32).reshape(128, NS, 512).transpose(1, 0, 2))
        angt = pos_own[:, None] * theta[None, :]
        ct = np.ones((NT, 4, 128, 128), f32)
        sn = np.zeros((NT, 4, 128, 128), f32)
        ct[:NS] = np.cos(angt).astype(f32).reshape(NS, 4, 128, 128)
        sn[:NS] = np.sin(angt).astype(f32).reshape(NS, 4, 128, 128)
        angf = fpos.astype(f32)[:, None] * theta[None, :]
        ct[NS + 1:] = np.cos(angf).astype(f32).reshape(NF, 4, 128, 128)
        sn[NS + 1:] = np.sin(angf).astype(f32).reshape(NF, 4, 128, 128)
        d["cosRT"] = np.ascontiguousarray(np.concatenate([ct, ct], axis=3))
        d["sinRT"] = np.ascontiguousarray(np.concatenate([sn, sn], axis=3))
        pos_all = np.concatenate([pos_own, pos_ext])
        row = np.floor(pos_all / cfg.GRID_W).astype(f32)
        col = (pos_all - row * cfg.GRID_W).astype(f32)
        ar = row[None, :] * inv[:, None]
        ac = col[None, :] * inv[:, None]
        cr, sr, cc, sc = np.cos(ar).astype(f32), np.sin(ar).astype(f32), np.cos(ac).astype(f32), np.sin(ac).astype(f32)
        cosA = np.concatenate([cr, cr, cc, cc], axis=0)
        sinA = np.concatenate([-sr, sr, -sc, sc], axis=0)
        cA = np.ones((NS + 1, 128, 512), f32)
        sA = np.zeros((NS + 1, 128, 512), f32)
        cA[:NS] = cosA[:, :T].reshape(128, NS, 512).transpose(1, 0, 2)
        sA[:NS] = sinA[:, :T].reshape(128, NS, 512).transpose(1, 0, 2)
        cA[NS, :, :256] = cosA[:, T:]
        sA[NS, :, :256] = sinA[:, T:]
        d["cosA"], d["sinA"] = cA, sA
        a = np.arange(128)[:, None]
        w = np.arange(384)[None, :]
        band = np.abs(w - 128 - a) <= 128
        mk = []
        for which in range(3):
            v = band.copy()
            if which == 0 and j == 0:
                v &= (w >= 128)
            if which == 2 and j == 3:
                v &= (w < 256)
            mk.append(np.where(v, 0.0, NEG).astype(f32))
        d["amask"] = np.ascontiguousarray(np.concatenate(mk, axis=1))
        NE = NFB + 1
        E = np.zeros((2, NE), f32)
        Mk = np.zeros((2, NE), f32)
        gbs = np.concatenate([np.arange(sg * NB, (sg + 1) * NB) for sg in range(4) if sg != j])
        for ci, gb in enumerate(gbs):
            if gb < j * NB:
                E[0, ci] = 128.0 * (j * NB - 1 - gb)
                Mk[0, ci] = 1
            else:
                E[1, ci] = 128.0 * (gb - (j + 1) * NB)
                Mk[1, ci] = 1
        E[0, NFB], Mk[0, NFB] = 128.0 * j * NB, 1
        E[1, NFB], Mk[1, NFB] = 128.0 * (4 * NB - (j + 1) * NB), 1
        d["etab"] = np.ascontiguousarray(np.broadcast_to(np.concatenate([E.reshape(-1), Mk.reshape(-1)])[None, :], (128, 4 * NE))).astype(f32)
        per_core.append(d)
    return per_core


class WStream:
    def __init__(self, P, slots, plan=None):
        self.P = P
        self.slots = slots
        self.n = len(slots)
        self.plan = plan
        self.req = []
        self.issued = 0
        self.k = 0

    def get(self, src, ncols):
        k = self.k
        self.k += 1
        if self.plan is None:
            self.req.append((src, ncols))
            return self.slots[k % self.n][:, 0:ncols], ("wr", k % self.n)
        lim = min(k + self.n, len(self.plan))
        while self.issued < lim:
            i = self.issued
            s, nco = self.plan[i]
            sl = i % self.n
            self.P.dma("pool", self.slots[sl][:, 0:nco], s, writes=[("wr", sl)], semkey=("wr", sl))
            self.issued += 1
        return self.slots[k % self.n][:, 0:ncols], ("wr", k % self.n)


def build(cfg, upto=99, dbg=False):
    nc = bass.Bass("TRN2", target_bir_lowering=False)
    D, DC, FC, T, NS, NB, RH, AH, KVH = cfg.D, cfg.DC, cfg.FC, cfg.T, cfg.NS, cfg.NB, cfg.RH, cfg.AH, cfg.KVH
    NF, NT, NFB = cfg.NF, cfg.NT, cfg.NFB
    fmcl = fm_chunk_list(cfg)
    tmbl = tm_block_list(cfg)

    def din(name, shape, dt=F32):
        return nc.dram_tensor(name, list(shape), dt, kind="ExternalInput").ap()

    def dscr(name, shape, dt=F32):
        if dbg:
            return nc.dram_tensor(name, list(shape), dt, kind="ExternalOutput").ap()
        return nc.dram_tensor(name, list(shape), dt).ap()

    I = {}
    I["xT"] = din("xT", [NT, DC, 128, 512])
    I["cT"] = din("cT", [128, DC * 2])
    I["wada"] = din("wada", [9 * DC, 128, DC * 128])
    for nm in ("1", "2"):
        I["wgu" + nm] = din("wgu" + nm, [2 * FC, 128, DC * 128])
        I["wdn" + nm] = din("wdn" + nm, [DC, 128, FC * 128])
    I["winfm"] = din("winfm", [len(fmcl), 128, DC * 128])
    tm_kcg = []
    for i, (kind, idx, c0, w) in enumerate(tmbl):
        kcg = min(cfg.WSLOT // w, DC)
        tm_kcg.append(kcg)
        I["wintm%d" % i] = din("wintm%d" % i, [DC // kcg, 128, kcg * w])
    I["wrup"] = din("wrup", [DC, 128, 2 * RH * 128])
    I["waup"] = din("waup", [DC, 128, AH * 128])
    I["wout"] = din("wout", [DC, 128, DC * 128])
    for nm, n in (("bada", 9 * DC), ("gpre", 3 * DC), ("gpost", 3 * DC), ("rate", 2 * RH), ("gn", 2 * RH),
                  ("sink", AH), ("dtab", 512), ("qidx", 2 * T), ("kcol", 3), ("amask", 3 * 384), ("etab", 4 * (NFB + 1)),
                  ("ident", 128)):
        I[nm] = din(nm, [128, n])
    I["cosR"] = din("cosR", [NS, 128, 512])
    I["sinR"] = din("sinR", [NS, 128, 512])
    I["cosRT"] = din("cosRT", [NT, 4, 128, 256])
    I["sinRT"] = din("sinRT", [NT, 4, 128, 256])
    I["cosA"] = din("cosA", [NS + 1, 128, 512])
    I["sinA"] = din("sinA", [NS + 1, 128, 512])
    outT = nc.dram_tensor("outT", [NS, DC, 128, 512], F32, kind="ExternalOutput").ap()
    DBG = {}
    if dbg:
        DBG["sc"] = nc.dram_tensor("d_sc", [128, 9 * 2 * DC], F32, kind="ExternalOutput").ap()
        DBG["xm"] = nc.dram_tensor("d_xm", [128, DC * 512], BF16, kind="ExternalOutput").ap()
        DBG["ht"] = nc.dram_tensor("d_ht", [128, FC * 512], BF16, kind="ExternalOutput").ap()
        DBG["y"] = nc.dram_tensor("d_y", [128, DC * 512], F32, kind="ExternalOutput").ap()
        DBG["rs"] = nc.dram_tensor("d_rs", [128, 512], F32, kind="ExternalOutput").ap()

    S = {}
    S["h1"] = dscr("s_h1", [NS, DC, 128, 512])
    S["h2"] = dscr("s_h2", [NS, DC, 128, 512])
    S["xl"] = dscr("s_xl", [NT, DC, 128, 512], BF16)
    S["rq"] = dscr("s_rq", [RH, 2, 128, T], BF16)
    S["rk"] = dscr("s_rk", [RH, 2, 128, T], BF16)
    S["rg"] = dscr("s_rg", [2 * RH, 128, T])
    S["rktm"] = dscr("s_rktm", [NB + 4 + NFB, 128, RH * 256], BF16)
    S["rvtm"] = dscr("s_rvtm", [NB + 4 + NFB, 128, RH * 256], BF16)
    S["aq"] = dscr("s_aq", [AH, 128, T], BF16)
    S["ak"] = dscr("s_ak", [KVH, 128, T + 256], BF16)
    S["cak"] = dscr("s_cak", [KVH, 128, 256], BF16)
    S["av"] = dscr("s_av", [NB + 2, 128, KVH * 128], BF16)
    S["cav"] = dscr("s_cav", [2, 128, KVH * 128], BF16)
    S["ga"] = dscr("s_ga", [DC, 128, T])
    S["gb"] = dscr("s_gb", [DC, 128, T])
    S["retl"] = dscr("s_retl", [2 * RH, 128, T], BF16)
    S["attl"] = dscr("s_attl", [AH, 128, T], BF16)

    with contextlib.ExitStack() as st:
        NW = 53000
        big = st.enter_context(nc.sbuf_tensor("big", [128, NW], F32))
        psb = [st.enter_context(nc.psum_tensor("ps%d" % i, [128, 512], F32)) for i in range(7)]
        pst = st.enter_context(nc.psum_tensor("pst", [128, 1024], BF16))

        class Mem:
            def __init__(self, base, size):
                self.base, self.size, self.o = base, size, base

            def f32(self, n):
                a = big[:, self.o:self.o + n]
                self.o += n
                assert self.o <= self.base + self.size, (self.o, self.base, self.size)
                return a

            def bf(self, n):
                w = (n + 1) // 2
                a = big[:, self.o:self.o + w].bitcast(BF16)
                self.o += w
                assert self.o <= self.base + self.size, (self.o, self.base, self.size)
                return a[:, 0:n]

            def reset(self):
                self.o = self.base

        perm = Mem(0, 1400 + 9 * 2 * DC + 6 * (NFB + 1))
        SC = perm.f32(9 * 2 * DC)
        ONESM = perm.f32(128)
        LG = perm.f32(2 * RH)
        GNT = perm.f32(2 * RH)
        SINK = perm.f32(AH)
        KCOL = perm.f32(3)
        ETAB = perm.f32(4 * (NFB + 1))
        IDENT = perm.bf(128)
        HS = perm.f32(8)
        CO = perm.f32(2 * (NFB + 1))
        COLS = perm.f32(64)
        small = Mem(perm.base + perm.size, 4096)
        assert perm.o <= perm.size
        RS = [small.f32(512), small.f32(512)]
        ACC = small.f32(512)
        SQ = small.f32(1024)
        TMP = [small.f32(512) for _ in range(3)]
        wr0 = small.base + small.size
        WR = []
        for i in range(cfg.NWS):
            WR.append(big[:, wr0 + i * (cfg.WSLOT // 2): wr0 + (i + 1) * (cfg.WSLOT // 2)].bitcast(BF16))
        r1_0 = wr0 + cfg.NWS * cfg.WSLOT // 2
        R1 = Mem(r1_0, DC * 512)
        R2 = Mem(r1_0 + DC * 512, NW - (r1_0 + DC * 512))
        RM = Mem(r1_0, NW - r1_0)
        assert R2.size >= FC * 256, (R2.size, FC * 256)
        NOPOOL = ("pe", "act", "dve", "sp")

        def sc(k, var, fc):
            i = (k * 2 + var) * DC + fc
            return SC[:, i:i + 1]

        def v3(ap, a):
            return ap.rearrange("p (a b) -> p a b", a=a)

        def run_pass(P, ws):
            bank_rr = [0]

            def nbank(n=6):
                b = bank_rr[0] % n
                bank_rr[0] += 1
                return b

            def stats_rstd(chunks, rs_out, rs_res, div, ncols=512):
                n = len(chunks)
                first = True
                g = 0
                while g < n:
                    grp = chunks[g:g + 2]
                    g += 2
                    for i, (ap, r) in enumerate(grp):
                        P.op("act", lambda e, ap=ap, i=i: e.activation(out=SQ[:, i * 512:i * 512 + ncols], in_=ap, func=AF.Square),
                             reads=(list(r) if isinstance(r, list) else [r]), writes=[("sq", i)])
                    if len(grp) == 2:
                        if first:
                            P.op("dve", lambda e: e.tensor_tensor(out=ACC[:, 0:ncols], in0=SQ[:, 0:ncols], in1=SQ[:, 512:512 + ncols], op=ALU.add),
                                 reads=[("sq", 0), ("sq", 1)], writes=["acc"])
                        else:
                            P.op("dve", lambda e: e.tensor_tensor(out=SQ[:, 0:ncols], in0=SQ[:, 0:ncols], in1=SQ[:, 512:512 + ncols], op=ALU.add),
                                 reads=[("sq", 0), ("sq", 1)], writes=[("sq", 0)])
                            P.op("dve", lambda e: e.tensor_tensor(out=ACC[:, 0:ncols], in0=ACC[:, 0:ncols], in1=SQ[:, 0:ncols], op=ALU.add),
                                 reads=[("sq", 0), "acc"], writes=["acc"])
                    else:
                        if first:
                            P.op("dve", lambda e: e.tensor_copy(out=ACC[:, 0:ncols], in_=SQ[:, 0:ncols]), reads=[("sq", 0)], writes=["acc"])
                        else:
                            P.op("dve", lambda e: e.tensor_tensor(out=ACC[:, 0:ncols], in0=ACC[:, 0:ncols], in1=SQ[:, 0:ncols], op=ALU.add),
                                 reads=[("sq", 0), "acc"], writes=["acc"])
                    first = False
                P.op("pe", lambda e: e.matmul(psb[6][:, 0:ncols], ONESM, ACC[:, 0:ncols], start=True, stop=True),
                     reads=["acc", "ones"], writes=[("ps", 6)])
                P.op("dve", lambda e: e.tensor_scalar(out=rs_out[:, 0:ncols], in0=psb[6][:, 0:ncols], scalar1=1.0 / div, scalar2=EPS, op0=ALU.mult, op1=ALU.add),
                     reads=[("ps", 6)], writes=[rs_res])
                P.op("act", lambda e: e.activation(out=rs_out[:, 0:ncols], in_=rs_out[:, 0:ncols], func=AF.Sqrt), reads=[rs_res], writes=[rs_res])
                P.op("dve", lambda e: e.reciprocal(out=rs_out[:, 0:ncols], in_=rs_out[:, 0:ncols]), reads=[rs_res], writes=[rs_res])

            def stats_add(ap, res, i, first, add_eng):
                sl = i % 2
                sq = SQ[:, sl * 512:(sl + 1) * 512]
                P.op("act", lambda e: e.activation(out=sq, in_=ap, func=AF.Square),
                     reads=(list(res) if isinstance(res, list) else [res]), writes=[("sq", sl)])
                if first:
                    P.op(add_eng, lambda e: e.tensor_copy(out=ACC, in_=sq), reads=[("sq", sl)], writes=["acc"])
                else:
                    P.op(add_eng, lambda e: e.tensor_tensor(out=ACC, in0=ACC, in1=sq, op=ALU.add), reads=[("sq", sl), "acc"], writes=["acc"])

            def stats_finish(rs_out, rs_res, div):
                P.op("pe", lambda e: e.matmul(psb[6][:], ONESM, ACC, start=True, stop=True),
                     reads=["acc", "ones"], writes=[("ps", 6)])
                P.op("dve", lambda e: e.tensor_scalar(out=rs_out, in0=psb[6][:], scalar1=1.0 / div, scalar2=EPS, op0=ALU.mult, op1=ALU.add),
                     reads=[("ps", 6)], writes=[rs_res])
                P.op("act", lambda e: e.activation(out=rs_out, in_=rs_out, func=AF.Sqrt), reads=[rs_res], writes=[rs_res])
                P.op("dve", lambda e: e.reciprocal(out=rs_out, in_=rs_out), reads=[rs_res], writes=[rs_res])

            def mod_chunk(src, src_res, rs, rs_res, kA, kB, fc, segs, dst, dst_res, tmp_i):
                tm = TMP[tmp_i]
                P.op("dve", lambda e: e.tensor_tensor(out=tm, in0=src, in1=rs, op=ALU.mult),
                     reads=[src_res, rs_res], writes=[("tmp", tmp_i)])
                for (c0, c1, var) in segs:
                    P.op("act", lambda e, c0=c0, c1=c1, var=var: e.activation(
                        out=dst[:, c0:c1], in_=tm[:, c0:c1], func=AF.Identity, bias=sc(kB, var, fc), scale=sc(kA, var, fc)),
                        reads=[("tmp", tmp_i), "sc"], writes=[dst_res])

            def phase0():
                R1.reset()
                MR = R1.f32(9 * DC * 2)
                CT = R1.f32(DC * 2)
                SCT = R1.bf(DC * 2)
                BA = R1.f32(9 * DC)
                GP = R1.f32(3 * DC)
                GQ = R1.f32(3 * DC)
                RT = R1.f32(2 * RH)
                IDF = R1.f32(128)
                T1 = R1.f32(DC)
                for nm, dst in (("cT", CT), ("bada", BA), ("gpre", GP), ("gpost", GQ), ("rate", RT), ("gn", GNT),
                                ("sink", SINK), ("kcol", KCOL), ("etab", ETAB), ("ident", IDF)):
                    P.dma("sp", dst, I[nm], writes=[("c", nm)], semkey="c")
                P.barrier(NOPOOL)
                P.op("dve", lambda e: e.memset(ONESM, 1.0), writes=["ones"])
                P.op("dve", lambda e: e.tensor_copy(out=IDENT, in_=IDF), reads=[("c", "ident")], writes=["ident"])
                P.op("act", lambda e: e.activation(out=LG, in_=RT, func=AF.Exp), reads=[("c", "rate")], writes=["lg"])
                P.op("dve", lambda e: e.tensor_scalar(out=LG, in0=LG, scalar1=-1.0, scalar2=None, op0=ALU.mult), reads=["lg"], writes=["lg"])
                P.op("act", lambda e: e.activation(out=SCT, in_=CT, func=AF.Silu), reads=[("c", "cT")], writes=["sct"])
                noc = 9 * DC
                for oc in range(noc):
                    slot, sres = ws.get(I["wada"][oc], DC * 128)
                    bk, col = divmod(oc * 2, 512)

                    def mm(e, slot=slot, bk=bk, col=col):
                        for kc in range(DC):
                            ins = e.matmul(psb[bk][:, col:col + 2], slot[:, kc * 128:(kc + 1) * 128], SCT[:, kc * 2:kc * 2 + 2],
                                           start=(kc == 0), stop=(kc == DC - 1))
                        return ins
                    P.op("pe", mm, reads=[sres, "sct"], writes=[("ps", bk)])
                nb_used = (noc * 2 + 511) // 512
                for bk in range(nb_used):
                    n = min(512, noc * 2 - bk * 512)
                    P.op("act", lambda e, bk=bk, n=n: e.activation(out=MR[:, bk * 512:bk * 512 + n], in_=psb[bk][:, 0:n], func=AF.Identity),
                         reads=[("ps", bk)], writes=["mr"])
                MR4 = MR.rearrange("p (i f v) -> p i f v", i=9, f=DC, v=2)
                BA3 = BA.rearrange("p (i f) -> p i f", i=9)
                for var in range(2):
                    for i in range(9):
                        P.op("dve", lambda e, i=i, var=var: e.tensor_tensor(out=MR4[:, i, :, var], in0=MR4[:, i, :, var], in1=BA3[:, i, :], op=ALU.add),
                             reads=["mr", ("c", "bada")], writes=["mr"])
                    for k in range(3):
                        shift, scale, gate = MR4[:, 3 * k, :, var], MR4[:, 3 * k + 1, :, var], MR4[:, 3 * k + 2, :, var]
                        coef = 1.0 if k == 1 else 0.5
                        oA = ((3 * k) * 2 + var) * DC
                        oB = ((3 * k + 1) * 2 + var) * DC
                        oC = ((3 * k + 2) * 2 + var) * DC
                        P.op("dve", lambda e, scale=scale: e.tensor_scalar(out=T1, in0=scale, scalar1=1.0, scalar2=None, op0=ALU.add),
                             reads=["mr"], writes=["t1"])
                        P.op("dve", lambda e, k=k, oA=oA: e.tensor_tensor(out=SC[:, oA:oA + DC], in0=T1, in1=GP[:, k * DC:(k + 1) * DC], op=ALU.mult),
                             reads=["t1", ("c", "gpre")], writes=["sc"])
                        P.op("dve", lambda e, shift=shift, oB=oB: e.tensor_copy(out=SC[:, oB:oB + DC], in_=shift), reads=["mr"], writes=["sc"])
                        P.op("dve", lambda e, gate=gate, k=k, oC=oC, coef=coef: e.scalar_tensor_tensor(
                            out=SC[:, oC:oC + DC], in0=gate, scalar=coef, in1=GQ[:, k * DC:(k + 1) * DC], op0=ALU.mult, op1=ALU.mult),
                            reads=["mr", ("c", "gpost")], writes=["sc"])
                if dbg:
                    P.dma("sp", DBG["sc"], SC, reads=["sc"], semkey="dbgsc")
                P.barrier(NOPOOL)

            def ffn(src, wgu, wdn, k, tiles, dst_of, xl_out):
                kA, kB, kC = 3 * k, 3 * k + 1, 3 * k + 2
                for (s, segs) in tiles:
                    R1.reset()
                    R2.reset()
                    X = v3(R2.f32(DC * 512), DC)
                    R2.reset()
                    HT = v3(R2.bf(FC * 512), FC)
                    R2.reset()
                    XL = v3(R2.bf(DC * 512), DC)
                    XM = v3(R1.bf(DC * 512), DC)
                    R1.reset()
                    Y = v3(R1.f32(DC * 512), DC)
                    g4 = max(1, DC // 4)
                    for q in range(0, DC, g4):
                        P.dma("sp", X[:, q:q + g4, :], src[s, q:q + g4].rearrange("f p t -> p f t"),
                              writes=[("x", f) for f in range(q, q + g4)], semkey=("xld", q // g4))
                    stats_rstd([(X[:, f, :], ("x", f)) for f in range(DC)], RS[0], "rs0", D)
                    for f in range(DC):
                        mod_chunk(X[:, f, :], ("x", f), RS[0], "rs0", kA, kB, f, segs, XM[:, f, :], ("xm", f), f % 3)
                    if dbg and k == 0 and s == 0:
                        P.dma("sp", DBG["xm"], XM.rearrange("p a b -> p (a b)"), reads=[("xm", f) for f in range(DC)], semkey="dbgxm")
                        P.dma("sp", DBG["rs"], RS[0], reads=["rs0"], semkey="dbgrs")
                    P.barrier(NOPOOL)
                    for j in range(FC):
                        b0 = 2 * (j % 3)
                        for wi, bk in ((2 * j, b0), (2 * j + 1, b0 + 1)):
                            slot, sres = ws.get(wgu[wi], DC * 128)

                            def mm(e, slot=slot, bk=bk):
                                for kc in range(DC):
                                    ins = e.matmul(psb[bk][:], slot[:, kc * 128:(kc + 1) * 128], XM[:, kc, :],
                                                   start=(kc == 0), stop=(kc == DC - 1))
                                return ins
                            P.op("pe", mm, reads=[sres] + [("xm", f) for f in range(DC)], writes=[("ps", bk)])
                        ti = j % 3
                        P.op("act", lambda e, b0=b0, ti=ti: e.activation(out=TMP[ti], in_=psb[b0][:], func=AF.Silu),
                             reads=[("ps", b0)], writes=[("tmp", ti)])
                        P.op("dve", lambda e, b0=b0, ti=ti, j=j: e.tensor_tensor(out=HT[:, j, :], in0=TMP[ti], in1=psb[b0 + 1][:], op=ALU.mult),
                             reads=[("tmp", ti), ("ps", b0 + 1)], writes=[("ht", j)])
                    if dbg and k == 0 and s == 0:
                        P.dma("sp", DBG["ht"], HT.rearrange("p a b -> p (a b)"), reads=[("ht", f) for f in range(FC)], semkey="dbght")
                    P.barrier(NOPOOL)
                    for f in range(DC):
                        bk = nbank()
                        pieces = []
                        q = 0
                        while q < FC:
                            n = min(32, FC - q)
                            pieces.append((q, n))
                            q += n
                        for (q, n) in pieces:
                            slot, sres = ws.get(wdn[f][:, q * 128:(q + n) * 128], n * 128)

                            def mm(e, slot=slot, q=q, n=n, bk=bk):
                                for kk in range(n):
                                    ins = e.matmul(psb[bk][:], slot[:, kk * 128:(kk + 1) * 128], HT[:, q + kk, :],
                                                   start=(q + kk == 0), stop=(q + kk == FC - 1))
                                return ins
                            P.op("pe", mm, reads=[sres] + [("ht", q + kk) for kk in range(n)], writes=[("ps", bk)])
                        P.op("act", lambda e, bk=bk, f=f: e.activation(out=Y[:, f, :], in_=psb[bk][:], func=AF.Identity),
                             reads=[("ps", bk)], writes=[("y", f)])
                        stats_add(Y[:, f, :], ("y", f), f, f == 0, "dve")
                    if dbg and k == 0 and s == 0:
                        P.dma("sp", DBG["y"], Y.rearrange("p a b -> p (a b)"), reads=[("y", f) for f in range(DC)], semkey="dbgy")
                        P.barrier(NOPOOL)
                    stats_finish(RS[1], "rs1", D)

                    def xld(f):
                        P.dma("sp", TMP[f % 3], src[s, f], writes=[("tmp", f % 3)], semkey=("tmpld", f % 3))
                    xld(0)
                    xld(1)
                    for f in range(DC):
                        ti = f % 3
                        if f + 2 < DC:
                            xld(f + 2)
                        P.op("dve", lambda e, f=f: e.tensor_tensor(out=Y[:, f, :], in0=Y[:, f, :], in1=RS[1], op=ALU.mult),
                             reads=[("y", f), "rs1"], writes=[("y", f)])
                        for (c0, c1, var) in segs:
                            P.op("dve", lambda e, f=f, c0=c0, c1=c1, var=var, ti=ti: e.scalar_tensor_tensor(
                                out=Y[:, f, c0:c1], in0=Y[:, f, c0:c1], scalar=sc(kC, var, f), in1=TMP[ti][:, c0:c1], op0=ALU.mult, op1=ALU.add),
                                reads=[("y", f), ("tmp", ti), "sc"], writes=[("y", f)])
                        if xl_out:
                            stats_add(Y[:, f, :], ("y", f), f, f == 0, "pool")
                        d = dst_of(s)
                        if d is not None:
                            P.dma("sp", d[f], Y[:, f, :], reads=[("y", f)], semkey=("yst", f % 4))
                    if xl_out:
                        stats_finish(RS[0], "rs0", D)
                        P.barrier(NOPOOL)
                        for f in range(DC):
                            mod_chunk(Y[:, f, :], ("y", f), RS[0], "rs0", 3, 4, f, segs, XL[:, f, :], ("xl", f), f % 3)
                        for q in range(0, DC, g4):
                            P.dma("sp", S["xl"][s, q:q + g4].rearrange("f p t -> p f t"), XL[:, q:q + g4, :],
                                  reads=[("xl", f) for f in range(q, q + g4)], semkey=("xlst", q // g4))
                    P.barrier(NOPOOL)

            def inproj():
                for s in range(NT):
                    extra = (s == NS)
                    foreign = (s > NS)
                    RM.reset()
                    XL = v3(RM.bf(DC * 512), DC)
                    COSR, SINR = RM.f32(512), RM.f32(512)
                    COSA, SINA = RM.f32(512), RM.f32(512)
                    CRT = RM.f32(4 * 256)
                    SRT = RM.f32(4 * 256)
                    XS = [RM.f32(512) for _ in range(4)]
                    T4 = [RM.f32(512) for _ in range(4)]
                    OB = [RM.bf(512) for _ in range(4)]
                    OF = [RM.f32(512) for _ in range(4)]
                    g4 = max(1, DC // 4)
                    for q in range(0, DC, g4):
                        P.dma("sp", XL[:, q:q + g4, :], S["xl"][s, q:q + g4].rearrange("f p t -> p f t"),
                              writes=[("xl", f) for f in range(q, q + g4)], semkey=("xld", q // g4))
                    if not extra and not foreign:
                        P.dma("sp", COSR, I["cosR"][s], writes=["cosr"], semkey="cosr")
                        P.dma("sp", SINR, I["sinR"][s], writes=["sinr"], semkey="sinr")
                    if not foreign:
                        P.dma("sp", COSA, I["cosA"][s], writes=["cosa"], semkey="cosa")
                        P.dma("sp", SINA, I["sinA"][s], writes=["sina"], semkey="sina")
                    P.dma("sp", v3(CRT, 4), I["cosRT"][s].rearrange("b p c -> p b c"), writes=["crt"], semkey="crt")
                    P.dma("sp", v3(SRT, 4), I["sinRT"][s].rearrange("b p c -> p b c"), writes=["srt"], semkey="srt")
                    ob_rr = [0]
                    of_rr = [0]
                    xs_rr = [0]

                    def fm_mm(ci):
                        slot, sres = ws.get(I["winfm"][ci], DC * 128)
                        bk = nbank()

                        def mm(e, slot=slot, bk=bk):
                            for kc in range(DC):
                                ins = e.matmul(psb[bk][:], slot[:, kc * 128:(kc + 1) * 128], XL[:, kc, :],
                                               start=(kc == 0), stop=(kc == DC - 1))
                            return ins
                        P.op("pe", mm, reads=[sres] + [("xl", f) for f in range(DC)], writes=[("ps", bk)])
                        return bk

                    def store_bf(src_i, dst):
                        P.dma("sp", dst, OB[src_i], reads=[("ob", src_i)], semkey=("obst", src_i))

                    ci = 0
                    while ci < len(fmcl):
                        kind, idx, _ = fmcl[ci]
                        gsz = 2 if kind in ("rq", "rk", "aq", "ak") else 1
                        if foreign or (extra and kind != "ak"):
                            ci += gsz
                            continue
                        if kind in ("rq", "rk"):
                            h = idx[0]
                            ba = fm_mm(ci)
                            bb = fm_mm(ci + 1)
                            ci += 2
                            scl = 1.0 if kind == "rq" else 0.0625
                            x1, x2 = xs_rr[0] % 4, (xs_rr[0] + 1) % 4
                            xs_rr[0] += 2
                            P.op("act", lambda e, ba=ba, x1=x1: e.activation(out=XS[x1], in_=psb[ba][:], func=AF.Identity),
                                 reads=[("ps", ba)], writes=[("xs", x1)])
                            P.op("act", lambda e, bb=bb, x2=x2: e.activation(out=XS[x2], in_=psb[bb][:], func=AF.Identity),
                                 reads=[("ps", bb)], writes=[("xs", x2)])
                            for half in range(2):
                                a, b_ = (x1, x2) if half == 0 else (x2, x1)
                                t0, t1 = 2 * half, 2 * half + 1
                                P.op("dve", lambda e, a=a, t0=t0, scl=scl: e.scalar_tensor_tensor(
                                    out=T4[t0], in0=XS[a], scalar=scl, in1=COSR, op0=ALU.mult, op1=ALU.mult),
                                    reads=[("xs", a), "cosr"], writes=[("t4", t0)])
                                P.op("dve", lambda e, b_=b_, t1=t1, scl=scl: e.scalar_tensor_tensor(
                                    out=T4[t1], in0=XS[b_], scalar=scl, in1=SINR, op0=ALU.mult, op1=ALU.mult),
                                    reads=[("xs", b_), "sinr"], writes=[("t4", t1)])
                                o = ob_rr[0] % 4
                                ob_rr[0] += 1
                                P.op("dve", lambda e, t0=t0, t1=t1, o=o, half=half: e.tensor_tensor(
                                    out=OB[o], in0=T4[t0], in1=T4[t1], op=(ALU.subtract if half == 0 else ALU.add)),
                                    reads=[("t4", t0), ("t4", t1)], writes=[("ob", o)])
                                store_bf(o, S[kind][h, half, :, s * 512:(s + 1) * 512])
                        elif kind in ("aq", "ak"):
                            ba = fm_mm(ci)
                            bb = fm_mm(ci + 1)
                            ci += 2
                            x1, x2 = xs_rr[0] % 4, (xs_rr[0] + 1) % 4
                            xs_rr[0] += 2
                            P.op("act", lambda e, ba=ba, x1=x1: e.activation(out=XS[x1], in_=psb[ba][:], func=AF.Identity),
                                 reads=[("ps", ba)], writes=[("xs", x1)])
                            P.op("act", lambda e, bb=bb, x2=x2: e.activation(out=XS[x2], in_=psb[bb][:], func=AF.Identity),
                                 reads=[("ps", bb)], writes=[("xs", x2)])
                            P.op("dve", lambda e, x1=x1: e.tensor_tensor(out=T4[0], in0=XS[x1], in1=COSA, op=ALU.mult),
                                 reads=[("xs", x1), "cosa"], writes=[("t4", 0)])
                            P.op("dve", lambda e, x2=x2: e.tensor_tensor(out=T4[1], in0=XS[x2], in1=SINA, op=ALU.mult),
                                 reads=[("xs", x2), "sina"], writes=[("t4", 1)])
                            o = ob_rr[0] % 4
                            ob_rr[0] += 1
                            P.op("dve", lambda e, o=o: e.tensor_tensor(out=OB[o], in0=T4[0], in1=T4[1], op=ALU.add),
                                 reads=[("t4", 0), ("t4", 1)], writes=[("ob", o)])
                            if kind == "aq":
                                store_bf(o, S["aq"][idx, :, s * 512:(s + 1) * 512])
                            elif not extra:
                                store_bf(o, S["ak"][idx, :, 128 + s * 512:128 + (s + 1) * 512])
                            else:
                                P.dma("sp", S["ak"][idx, :, 0:128], OB[o][:, 0:128], reads=[("ob", o)], semkey=("obst", o))
                                P.dma("sp", S["ak"][idx, :, T + 128:T + 256], OB[o][:, 128:256], reads=[("ob", o)], semkey=("obst", o))
                                P.dma("sp", S["cak"][idx], OB[o][:, 256:512], reads=[("ob", o)], semkey=("obst", o))
                        else:
                            bk = fm_mm(ci)
                            ci += 1
                            o = of_rr[0] % 4
                            of_rr[0] += 1
                            fn = AF.Silu if kind == "rg" else AF.Sigmoid
                            P.op("act", lambda e, bk=bk, o=o, fn=fn: e.activation(out=OF[o], in_=psb[bk][:], func=fn),
                                 reads=[("ps", bk)], writes=[("of", o)])
                            P.dma("sp", S[kind][idx, :, s * 512:(s + 1) * 512], OF[o], reads=[("of", o)], semkey=("ofst", o))
                    for bi, (kind, idx, c0, w) in enumerate(tmbl):
                        if foreign and kind == "av":
                            continue
                        kcg = tm_kcg[bi]
                        npc = DC // kcg
                        for pc in range(npc):
                            slot, sres = ws.get(I["wintm%d" % bi][pc], kcg * w)

                            def mm(e, slot=slot, pc=pc, kcg=kcg, w=w, npc=npc):
                                for tb in range(4):
                                    for kk in range(kcg):
                                        kc = pc * kcg + kk
                                        ins = e.matmul(psb[tb][:, 0:w], XL[:, kc, tb * 128:(tb + 1) * 128], slot[:, kk * w:(kk + 1) * w],
                                                       start=(kc == 0), stop=(kc == DC - 1))
                                return ins
                            P.op("pe", mm, reads=[sres] + [("xl", f) for f in range(DC)], writes=[("ps", tb) for tb in range(4)])
                        for tb in range(4):
                            o = ob_rr[0] % 4
                            ob_rr[0] += 1
                            if kind == "rk":
                                xi = xs_rr[0] % 4
                                xs_rr[0] += 1
                                P.op("act", lambda e, tb=tb, xi=xi, w=w: e.activation(out=XS[xi][:, 0:w], in_=psb[tb][:, 0:w], func=AF.Identity),
                                     reads=[("ps", tb)], writes=[("xs", xi)])
                                g = w // 256
                                xv = XS[xi][:, 0:w].rearrange("p (g h d) -> p g h d", g=g, h=2)
                                ov = OB[o][:, 0:w].rearrange("p (g h d) -> p g h d", g=g, h=2)
                                cv = CRT[:, tb * 256:tb * 256 + g * 128].rearrange("p (g d) -> p g d", g=g)
                                sv = SRT[:, tb * 256:tb * 256 + g * 128].rearrange("p (g d) -> p g d", g=g)
                                tv = [T4[i][:, 0:g * 128].rearrange("p (g d) -> p g d", g=g) for i in range(4)]
                                for half in range(2):
                                    a, b_ = (0, 1) if half == 0 else (1, 0)
                                    t0, t1 = 2 * half, 2 * half + 1
                                    P.op("dve", lambda e, a=a, t0=t0, xv=xv, cv=cv, tv=tv: e.scalar_tensor_tensor(
                                        out=tv[t0], in0=xv[:, :, a, :], scalar=0.0625, in1=cv, op0=ALU.mult, op1=ALU.mult),
                                        reads=[("xs", xi), "crt"], writes=[("t4", t0)])
                                    P.op("dve", lambda e, b_=b_, t1=t1, xv=xv, sv=sv, tv=tv: e.scalar_tensor_tensor(
                                        out=tv[t1], in0=xv[:, :, b_, :], scalar=0.0625, in1=sv, op0=ALU.mult, op1=ALU.mult),
                                        reads=[("xs", xi), "srt"], writes=[("t4", t1)])
                                    P.op("dve", lambda e, t0=t0, t1=t1, half=half, ov=ov, tv=tv: e.tensor_tensor(
                                        out=ov[:, :, half, :], in0=tv[t0], in1=tv[t1], op=(ALU.subtract if half == 0 else ALU.add)),
                                        reads=[("t4", t0), ("t4", t1)], writes=[("ob", o)])
                            else:
                                P.op("act", lambda e, tb=tb, o=o, w=w: e.activation(out=OB[o][:, 0:w], in_=psb[tb][:, 0:w], func=AF.Identity),
                                     reads=[("ps", tb)], writes=[("ob", o)])
                            if kind in ("rk", "rv"):
                                blk = (s * 4 + tb) if s < NS else (NB + (s - NS) * 4 + tb)
                                dst = S[kind + "tm"][blk][:, idx * w:(idx + 1) * w]
                            else:
                                if not extra:
                                    dst = S["av"][1 + s * 4 + tb][:, idx * w:(idx + 1) * w]
                                elif tb == 0:
                                    dst = S["av"][0][:, idx * w:(idx + 1) * w]
                                elif tb == 1:
                                    dst = S["av"][NB + 1][:, idx * w:(idx + 1) * w]
                                else:
                                    dst = S["cav"][tb - 2][:, idx * w:(idx + 1) * w]
                            P.dma("sp", dst, OB[o][:, 0:w], reads=[("ob", o)], semkey=("obst", o))
                    P.barrier(NOPOOL)

            def retention():
                NE = NFB + 1
                for h in range(RH):
                    RM.reset()
                    SR = [RM.f32(512), RM.f32(512)]
                    mark = RM.o
                    lgf, lgb = LG[:, h:h + 1], LG[:, RH + h:RH + h + 1]
                    P.op("act", lambda e, lgf=lgf: e.activation(out=HS[:, 0:1], in_=KCOL[:, 0:1], func=AF.Exp, scale=lgf), reads=["lg", ("c", "kcol")], writes=["hs"])
                    P.op("act", lambda e, lgb=lgb: e.activation(out=HS[:, 1:2], in_=KCOL[:, 1:2], func=AF.Exp, scale=lgb), reads=["lg", ("c", "kcol")], writes=["hs"])
                    P.op("act", lambda e, lgf=lgf: e.activation(out=HS[:, 2:3], in_=KCOL[:, 2:3], func=AF.Exp, scale=lgf), reads=["lg", ("c", "kcol")], writes=["hs"])
                    P.op("act", lambda e, lgb=lgb: e.activation(out=HS[:, 3:4], in_=KCOL[:, 2:3], func=AF.Exp, scale=lgb), reads=["lg", ("c", "kcol")], writes=["hs"])
                    for d in range(2):
                        lg = lgf if d == 0 else lgb
                        P.op("act", lambda e, d=d, lg=lg: e.activation(out=CO[:, d * NE:(d + 1) * NE], in_=ETAB[:, d * NE:(d + 1) * NE], func=AF.Exp, scale=lg),
                             reads=["lg", ("c", "etab")], writes=["co"])
                        P.op("dve", lambda e, d=d: e.tensor_tensor(out=CO[:, d * NE:(d + 1) * NE], in0=CO[:, d * NE:(d + 1) * NE],
                                                                  in1=ETAB[:, 2 * NE + d * NE:2 * NE + (d + 1) * NE], op=ALU.mult),
                             reads=["co", ("c", "etab")], writes=["co"])

                    def umm(kd, kres, vv, vres, c):
                        bk = nbank()

                        def mm(e, c=c, bk=bk):
                            for e2 in range(2):
                                ins = e.matmul(psb[bk][:, e2 * 256:(e2 + 1) * 256], kd[:, c, e2 * 128:(e2 + 1) * 128], vv[:, c, :],
                                               start=True, stop=True)
                            return ins
                        P.op("pe", mm, reads=[kres, vres], writes=[("ps", bk)])
                        return bk

                    def recur(kd, kres, vv, vres, order, d, have, snap=None):
                        Sd = SR[d]
                        g = HS[:, 2 + d:3 + d]
                        for c in order:
                            if snap is not None:
                                snap(c, Sd)
                            bk = umm(kd, kres, vv, vres, c)
                            if not have:
                                P.op("dve", lambda e, bk=bk: e.tensor_copy(out=Sd, in_=psb[bk][:]), reads=[("ps", bk)], writes=[("sr", d)])
                                have = True
                            else:
                                P.op("dve", lambda e, bk=bk: e.scalar_tensor_tensor(out=Sd, in0=Sd, scalar=g, in1=psb[bk][:], op0=ALU.mult, op1=ALU.add),
                                     reads=[("ps", bk), ("sr", d), "hs"], writes=[("sr", d)])

                    KF = v3(RM.bf(NFB * 256), NFB)
                    VF = v3(RM.bf(NFB * 256), NFB)
                    KFD = v3(RM.bf(NFB * 256), NFB)
                    CK = v3(RM.bf(512), 2)
                    CV = v3(RM.bf(512), 2)
                    CKF = v3(RM.bf(512), 2)
                    CKB = v3(RM.bf(512), 2)
                    hs_ = slice(h * 256, (h + 1) * 256)
                    P.dma("sp", KF, S["rktm"][NB + 4:NB + 4 + NFB, :, hs_].rearrange("b p c -> p b c"), writes=["kf"], semkey="kf")
                    P.dma("sp", VF, S["rvtm"][NB + 4:NB + 4 + NFB, :, hs_].rearrange("b p c -> p b c"), writes=["vf"], semkey="vf")
                    P.dma("sp", CK, S["rktm"][NB + 2:NB + 4, :, hs_].rearrange("b p c -> p b c"), writes=["ck"], semkey="ck")
                    P.dma("sp", CV, S["rvtm"][NB + 2:NB + 4, :, hs_].rearrange("b p c -> p b c"), writes=["cv"], semkey="cv")
                    P.op("dve", lambda e: e.tensor_scalar(out=KFD, in0=KF, scalar1=HS[:, 0:1], scalar2=None, op0=ALU.mult), reads=["kf", "hs"], writes=["kfd"])
                    P.op("pool", lambda e: e.tensor_scalar(out=KF, in0=KF, scalar1=HS[:, 1:2], scalar2=None, op0=ALU.mult), reads=["kf", "hs", "kfd"], writes=["kf"])
                    P.op("dve", lambda e: e.tensor_scalar(out=CKF, in0=CK, scalar1=HS[:, 0:1], scalar2=None, op0=ALU.mult), reads=["ck", "hs"], writes=["ckf"])
                    P.op("dve", lambda e: e.tensor_scalar(out=CKB, in0=CK, scalar1=HS[:, 1:2], scalar2=None, op0=ALU.mult), reads=["ck", "hs"], writes=["ckb"])
                    recur(CKF, "ckf", CV, "cv", [0, 1], 0, False)
                    recur(CKB, "ckb", CV, "cv", [1, 0], 1, False)
                    for d in range(2):
                        cc = CO[:, d * NE + NFB:d * NE + NFB + 1]
                        P.op("dve", lambda e, d=d, cc=cc: e.tensor_scalar(out=SR[d], in0=SR[d], scalar1=cc, scalar2=None, op0=ALU.mult),
                             reads=[("sr", d), "co"], writes=[("sr", d)])
                    for c in range(NFB):
                        for d in range(2):
                            bk = umm(KFD if d == 0 else KF, "kfd" if d == 0 else "kf", VF, "vf", c)
                            cc = CO[:, d * NE + c:d * NE + c + 1]
                            P.op("dve", lambda e, d=d, cc=cc, bk=bk: e.scalar_tensor_tensor(out=SR[d], in0=psb[bk][:], scalar=cc, in1=SR[d], op0=ALU.mult, op1=ALU.add),
                                 reads=[("ps", bk), ("sr", d), "co"], writes=[("sr", d)])
                    P.barrier()

                    RM.o = mark
                    KTM = v3(RM.bf(NB * 256), NB)
                    VTM = v3(RM.bf(NB * 256), NB)
                    KDF = v3(RM.bf(NB * 256), NB)
                    KDB = v3(RM.bf(NB * 256), NB)
                    P.dma("sp", KTM, S["rktm"][0:NB, :, hs_].rearrange("b p c -> p b c"), writes=["ktm"], semkey="ktm")
                    P.dma("sp", VTM, S["rvtm"][0:NB, :, hs_].rearrange("b p c -> p b c"), writes=["vtm"], semkey="vtm")
                    P.op("dve", lambda e: e.tensor_scalar(out=KDF, in0=KTM, scalar1=HS[:, 0:1], scalar2=None, op0=ALU.mult), reads=["ktm", "hs"], writes=["kdf"])
                    P.op("pool", lambda e: e.tensor_scalar(out=KDB, in0=KTM, scalar1=HS[:, 1:2], scalar2=None, op0=ALU.mult), reads=["ktm", "hs"], writes=["kdb"])
                    QT = v3(RM.bf(2 * T), 2)
                    KT = v3(RM.bf(2 * T), 2)
                    QF = v3(RM.bf(2 * T), 2)
                    QB = v3(RM.bf(2 * T), 2)
                    RG = v3(RM.f32(2 * T), 2)
                    OT = RM.f32(2 * T)
                    OT3 = v3(OT, 2)
                    SFB = [v3(RM.bf(512), 2) for _ in range(NB)]
                    SBB = [v3(RM.bf(512), 2) for _ in range(NB)]
                    DTR = RM.f32(512)
                    DT = RM.f32(128)
                    ATT = [RM.bf(128) for _ in range(3)]
                    OBR = [RM.bf(512) for _ in range(3)]
                    P.dma("sp", QT, S["rq"][h].rearrange("e p t -> p e t"), writes=["qt"], semkey="qt")
                    P.dma("sp", KT, S["rk"][h].rearrange("e p t -> p e t"), writes=["kt"], semkey="kt")
                    P.dma("sp", RG, S["rg"][2 * h:2 * h + 2].rearrange("e p t -> p e t"), writes=["rg"], semkey="rg")
                    P.dma("sp", OT, I["qidx"], writes=["ot"], semkey="otld")
                    P.dma("sp", DTR, I["dtab"], writes=["dtr"], semkey="dtr")

                    def snapf(c, Sd):
                        P.op("act", lambda e, c=c: e.activation(out=SFB[c].rearrange("p a b -> p (a b)"), in_=Sd, func=AF.Identity),
                             reads=[("sr", 0)], writes=[("sfb", c)])

                    def snapb(c, Sd):
                        P.op("act", lambda e, c=c: e.activation(out=SBB[c].rearrange("p a b -> p (a b)"), in_=Sd, func=AF.Identity),
                             reads=[("sr", 1)], writes=[("sbb", c)])
                    recur(KDF, "kdf", VTM, "vtm", list(range(NB)), 0, True, snapf)
                    recur(KDB, "kdb", VTM, "vtm", list(range(NB - 1, -1, -1)), 1, True, snapb)
                    P.op("act", lambda e, lgf=lgf: e.activation(out=OT[:, 0:T], in_=OT[:, 0:T], func=AF.Exp, scale=lgf), reads=["ot", "lg"], writes=["ot"])
                    P.op("act", lambda e, lgb=lgb: e.activation(out=OT[:, T:2 * T], in_=OT[:, T:2 * T], func=AF.Exp, scale=lgb), reads=["ot", "lg"], writes=["ot"])
                    for e2 in range(2):
                        P.op("dve", lambda e, e2=e2: e.tensor_tensor(out=QF[:, e2, :], in0=QT[:, e2, :], in1=OT[:, 0:T], op=ALU.mult),
                             reads=["qt", "ot"], writes=["qf"])
                        P.op("pool", lambda e, e2=e2: e.tensor_tensor(out=QB[:, e2, :], in0=QT[:, e2, :], in1=OT[:, T:2 * T], op=ALU.mult),
                             reads=["qt", "ot"], writes=["qb"])
                    P.op("act", lambda e, lgf=lgf: e.activation(out=DTR[:, 0:128], in_=DTR[:, 0:128], func=AF.Exp, scale=lgf), reads=["dtr", "lg"], writes=["dtr"])
                    P.op("act", lambda e, lgb=lgb: e.activation(out=DTR[:, 256:384], in_=DTR[:, 256:384], func=AF.Exp, scale=lgb), reads=["dtr", "lg"], writes=["dtr"])
                    P.op("dve", lambda e: e.tensor_tensor(out=DTR[:, 0:128], in0=DTR[:, 0:128], in1=DTR[:, 128:256], op=ALU.mult), reads=["dtr"], writes=["dtr"])
                    P.op("dve", lambda e: e.tensor_tensor(out=DTR[:, 256:384], in0=DTR[:, 256:384], in1=DTR[:, 384:512], op=ALU.mult), reads=["dtr"], writes=["dtr"])
                    P.op("dve", lambda e: e.tensor_tensor(out=DT, in0=DTR[:, 0:128], in1=DTR[:, 256:384], op=ALU.add), reads=["dtr"], writes=["dt"])
                    P.barrier()
                    for c in range(NB):
                        cs = slice(c * 128, (c + 1) * 128)
                        br = nbank()

                        def mmr(e, br=br, cs=cs):
                            for e2 in range(2):
                                ins = e.matmul(psb[br][:, 0:128], KT[:, e2, cs], QT[:, e2, cs], start=(e2 == 0), stop=(e2 == 1))
                            return ins
                        P.op("pe", mmr, reads=["kt", "qt"], writes=[("ps", br)])
                        ai = c % 3
                        P.op("dve", lambda e, br=br, ai=ai: e.tensor_tensor(out=ATT[ai], in0=psb[br][:, 0:128], in1=DT, op=ALU.mult),
                             reads=[("ps", br), "dt"], writes=[("att", ai)])
                        bo = nbank()

                        def mmo(e, bo=bo, c=c, cs=cs, ai=ai):
                            for e2 in range(2):
                                o = psb[bo][:, e2 * 128:(e2 + 1) * 128]
                                ds_ = slice(e2 * 128, (e2 + 1) * 128)
                                e.matmul(o, VTM[:, c, ds_], ATT[ai], start=True, stop=False)
                                for ee in range(2):
                                    e.matmul(o, SFB[c][:, ee, ds_], QF[:, ee, cs], start=False, stop=False)
                                for ee in range(2):
                                    ins = e.matmul(o, SBB[c][:, ee, ds_], QB[:, ee, cs], start=False, stop=(ee == 1))
                            return ins
                        P.op("pe", mmo, reads=["vtm", ("att", ai), ("sfb", c), ("sbb", c), "qf", "qb"], writes=[("ps", bo)])
                        P.op("act", lambda e, bo=bo, cs=cs: e.activation(out=OT3[:, :, cs], in_=psb[bo][:, 0:256].rearrange("p (a b) -> p a b", a=2), func=AF.Identity),
                             reads=[("ps", bo)], writes=[("otc", c)])
                    for q in range(NS):
                        ts_ = slice(q * 512, (q + 1) * 512)
                        rres = [("otc", c) for c in range(q * 4, q * 4 + 4)]
                        stats_rstd([(OT3[:, 0, ts_], rres), (OT3[:, 1, ts_], rres)], RS[0], "rs0", 256.0)
                        for e2 in range(2):
                            ti = e2
                            P.op("dve", lambda e, e2=e2, ts_=ts_, ti=ti: e.tensor_tensor(out=TMP[ti], in0=OT3[:, e2, ts_], in1=RS[0], op=ALU.mult),
                                 reads=rres + ["rs0"], writes=[("tmp", ti)])
                            oi = (q * 2 + e2) % 3
                            gcol = GNT[:, 2 * h + e2:2 * h + e2 + 1]
                            P.op("dve", lambda e, e2=e2, ts_=ts_, ti=ti, oi=oi, gcol=gcol: e.scalar_tensor_tensor(
                                out=OBR[oi], in0=TMP[ti], scalar=gcol, in1=RG[:, e2, ts_], op0=ALU.mult, op1=ALU.mult),
                                reads=[("tmp", ti), "rg", ("c", "gn")], writes=[("obr", oi)])
                            P.dma("sp", S["retl"][2 * h + e2, :, ts_], OBR[oi], reads=[("obr", oi)], semkey=("obrst", oi))
                    P.barrier()

            def attention():
                scale = 128.0 ** -0.5
                for g in range(KVH):
                    RM.reset()
                    AKE = RM.bf(T + 256)
                    CAK = RM.bf(256)
                    AVE = v3(RM.bf((NB + 2) * 128), NB + 2)
                    CAV = v3(RM.bf(256), 2)
                    MASK = RM.f32(3 * 384)
                    AQ = [RM.bf(T) for _ in range(2)]
                    AOT = [RM.bf(T) for _ in range(2)]
                    SS = [RM.f32(640) for _ in range(4)]
                    PB = [RM.bf(640) for _ in range(4)]
                    PT = [RM.bf(640) for _ in range(4)]
                    P.dma("sp", AKE, S["ak"][g], writes=["ake"], semkey="ake")
                    P.dma("sp", CAK, S["cak"][g], writes=["cak"], semkey="cak")
                    P.dma("sp", AVE, S["av"][:, :, g * 128:(g + 1) * 128].rearrange("b p c -> p b c"), writes=["ave"], semkey="ave")
                    P.dma("sp", CAV, S["cav"][:, :, g * 128:(g + 1) * 128].rearrange("b p c -> p b c"), writes=["cav"], semkey="cav")
                    P.dma("sp", MASK, I["amask"], writes=["mask"], semkey="mask")
                    for hq in range(4):
                        h = g * 4 + hq
                        qi = h % 2
                        P.dma("sp", AQ[qi], S["aq"][h], writes=[("aq", qi)], semkey=("aq", qi))
                        skc = SINK[:, h:h + 1]
                        for b in range(NB):
                            bs = slice(b * 128, (b + 1) * 128)
                            which = 0 if b == 0 else (2 if b == NB - 1 else 1)
                            si = b % 4
                            b1 = nbank()
                            b2 = nbank()

                            def mms(e, b1=b1, b2=b2, bs=bs, b=b, qi=qi):
                                e.matmul(psb[b1][:, 0:384], AQ[qi][:, bs], AKE[:, b * 128:b * 128 + 384], start=True, stop=True)
                                return e.matmul(psb[b2][:, 0:256], AQ[qi][:, bs], CAK, start=True, stop=True)
                            P.op("pe", mms, reads=[("aq", qi), "ake", "cak"], writes=[("ps", b1), ("ps", b2)])
                            P.op("dve", lambda e, b1=b1, si=si, which=which: e.scalar_tensor_tensor(
                                out=SS[si][:, 0:384], in0=psb[b1][:, 0:384], scalar=scale, in1=MASK[:, which * 384:(which + 1) * 384], op0=ALU.mult, op1=ALU.add),
                                reads=[("ps", b1), "mask"], writes=[("ss", si)])
                            P.op("act", lambda e, b2=b2, si=si: e.activation(out=SS[si][:, 384:640], in_=psb[b2][:, 0:256], func=AF.Identity, scale=scale),
                                 reads=[("ps", b2)], writes=[("ss", si)])
                            c0 = (b % 4) * 8
                            mx, nmx, sm, es, den = (COLS[:, c0 + i:c0 + i + 1] for i in range(5))
                            cres = ("cols", b % 4)
                            P.op("dve", lambda e, si=si, mx=mx: e.reduce_max(out=mx, in_=SS[si], axis=mybir.AxisListType.X), reads=[("ss", si)], writes=[cres])
                            P.op("dve", lambda e, mx=mx, skc=skc: e.tensor_tensor(out=mx, in0=mx, in1=skc, op=ALU.max), reads=[cres, ("c", "sink")], writes=[cres])
                            P.op("dve", lambda e, mx=mx, nmx=nmx: e.tensor_scalar(out=nmx, in0=mx, scalar1=-1.0, scalar2=None, op0=ALU.mult), reads=[cres], writes=[cres])
                            P.op("dve", lambda e, sm=sm: e.memset(sm, 0.0), writes=[cres])
                            P.op("act", lambda e, si=si, nmx=nmx, sm=sm: e.activation(out=SS[si], in_=SS[si], func=AF.Exp, bias=nmx, scale=1.0, accum_out=sm),
                                 reads=[("ss", si), cres], writes=[("ss", si), cres])
                            P.op("act", lambda e, nmx=nmx, es=es, skc=skc: e.activation(out=es, in_=skc, func=AF.Exp, bias=nmx, scale=1.0),
                                 reads=[cres, ("c", "sink")], writes=[cres])
                            P.op("dve", lambda e, sm=sm, es=es, den=den: e.tensor_tensor(out=den, in0=sm, in1=es, op=ALU.add), reads=[cres], writes=[cres])
                            P.op("dve", lambda e, den=den: e.reciprocal(out=den, in_=den), reads=[cres], writes=[cres])
                            P.op("dve", lambda e, si=si, den=den: e.tensor_scalar(out=PB[si], in0=SS[si], scalar1=den, scalar2=None, op0=ALU.mult),
                                 reads=[("ss", si), cres], writes=[("pb", si)])

                            def mmt(e, si=si):
                                for kc in range(5):
                                    ins = e.transpose(pst[:, kc * 128:(kc + 1) * 128], PB[si][:, kc * 128:(kc + 1) * 128], IDENT)
                                return ins
                            P.op("pe", mmt, reads=[("pb", si), "ident"], writes=["pst"])
                            P.op("act", lambda e, si=si: e.activation(out=PT[si], in_=pst[:, 0:640], func=AF.Identity), reads=["pst"], writes=[("pt", si)])
                            b3 = nbank()

                            def mmo(e, b3=b3, b=b, si=si):
                                for kc in range(5):
                                    vv = AVE[:, b + kc, :] if kc < 3 else CAV[:, kc - 3, :]
                                    ins = e.matmul(psb[b3][:, 0:128], vv, PT[si][:, kc * 128:(kc + 1) * 128], start=(kc == 0), stop=(kc == 4))
                                return ins
                            P.op("pe", mmo, reads=[("pt", si), "ave", "cav"], writes=[("ps", b3)])
                            P.op("act", lambda e, b3=b3, bs=bs, qi=qi: e.activation(out=AOT[qi][:, bs], in_=psb[b3][:, 0:128], func=AF.Identity),
                                 reads=[("ps", b3)], writes=[("aot", qi)])
                        P.dma("sp", S["attl"][h], AOT[qi], reads=[("aot", qi)], semkey=("aotst", qi))
                    P.barrier()

            def merge():
                for s in range(NS):
                    ts_ = slice(s * 512, (s + 1) * 512)
                    R1.reset()
                    R2.reset()
                    Y = v3(R1.f32(DC * 512), DC)
                    RL = v3(R2.bf(2 * RH * 512), 2 * RH)
                    AL = v3(R2.bf(AH * 512), AH)
                    Z = v3(R2.bf(DC * 512), DC)
                    GA = [R2.f32(512) for _ in range(2)]
                    GB = [R2.f32(512) for _ in range(2)]
                    P.dma("sp", RL, S["retl"][:, :, ts_].rearrange("c p t -> p c t"), writes=["rl"], semkey="rl")
                    P.dma("sp", AL, S["attl"][:, :, ts_].rearrange("c p t -> p c t"), writes=["al"], semkey="al")
                    for f in range(DC):
                        gi = f % 2
                        P.dma("sp", GA[gi], S["ga"][f, :, ts_], writes=[("ga", gi)], semkey=("ga", gi))
                        P.dma("sp", GB[gi], S["gb"][f, :, ts_], writes=[("gb", gi)], semkey=("gb", gi))
                        ba, bb = nbank(), nbank()

                        s1, r1 = ws.get(I["wrup"][f], 2 * RH * 128)

                        def mma(e, s1=s1, ba=ba):
                            for kc in range(2 * RH):
                                ins = e.matmul(psb[ba][:], s1[:, kc * 128:(kc + 1) * 128], RL[:, kc, :], start=(kc == 0), stop=(kc == 2 * RH - 1))
                            return ins

                        P.op("pe", mma, reads=[r1, "rl"], writes=[("ps", ba)])
                        s2, r2 = ws.get(I["waup"][f], AH * 128)

                        def mmb(e, s2=s2, bb=bb):
                            for kc in range(AH):
                                ins = e.matmul(psb[bb][:], s2[:, kc * 128:(kc + 1) * 128], AL[:, kc, :], start=(kc == 0), stop=(kc == AH - 1))
                            return ins
                        P.op("pe", mmb, reads=[r2, "al"], writes=[("ps", bb)])
                        P.op("dve", lambda e, gi=gi, ba=ba: e.tensor_tensor(out=GA[gi], in0=GA[gi], in1=psb[ba][:], op=ALU.mult),
                             reads=[("ga", gi), ("ps", ba)], writes=[("ga", gi)])
                        P.op("dve", lambda e, gi=gi, bb=bb: e.tensor_tensor(out=GB[gi], in0=GB[gi], in1=psb[bb][:], op=ALU.mult),
                             reads=[("gb", gi), ("ps", bb)], writes=[("gb", gi)])
                        P.op("dve", lambda e, gi=gi, f=f: e.tensor_tensor(out=Z[:, f, :], in0=GA[gi], in1=GB[gi], op=ALU.add),
                             reads=[("ga", gi), ("gb", gi)], writes=[("z", f)])
                    for f in range(DC):
                        slot, sres = ws.get(I["wout"][f], DC * 128)
                        bk = nbank()

                        def mm(e, slot=slot, bk=bk):
                            for kc in range(DC):
                                ins = e.matmul(psb[bk][:], slot[:, kc * 128:(kc + 1) * 128], Z[:, kc, :], start=(kc == 0), stop=(kc == DC - 1))
                            return ins
                        P.op("pe", mm, reads=[sres] + [("z", k) for k in range(DC)], writes=[("ps", bk)])
                        P.op("act", lambda e, bk=bk, f=f: e.activation(out=Y[:, f, :], in_=psb[bk][:], func=AF.Identity),
                             reads=[("ps", bk)], writes=[("y", f)])
                        stats_add(Y[:, f, :], ("y", f), f, f == 0, "dve")
                    stats_finish(RS[1], "rs1", D)

                    def hld(f):
                        P.dma("sp", TMP[f % 3], S["h1"][s, f], writes=[("tmp", f % 3)], semkey=("tmpld", f % 3))
                    hld(0)
                    hld(1)
                    for f in range(DC):
                        ti = f % 3
                        if f + 2 < DC:
                            hld(f + 2)
                        P.op("dve", lambda e, f=f: e.tensor_tensor(out=Y[:, f, :], in0=Y[:, f, :], in1=RS[1], op=ALU.mult),
                             reads=[("y", f), "rs1"], writes=[("y", f)])
                        P.op("dve", lambda e, f=f, ti=ti: e.scalar_tensor_tensor(
                            out=Y[:, f, :], in0=Y[:, f, :], scalar=sc(5, 0, f), in1=TMP[ti], op0=ALU.mult, op1=ALU.add),
                            reads=[("y", f), ("tmp", ti), "sc"], writes=[("y", f)])
                        P.dma("sp", S["h2"][s, f], Y[:, f, :], reads=[("y", f)], semkey=("yst", f % 4))
                    P.barrier(NOPOOL)

            own = [(s, [(0, 512, 0)]) for s in range(NS)]
            ext = [(NS, [(0, 256, 0), (256, 512, 1)])]
            frn = [(NS + 1 + i, [(0, 512, 0)]) for i in range(NF)]
            phase0()
            if upto >= 1:
                ffn(I["xT"], I["wgu1"], I["wdn1"], 0, own + ext + frn, lambda s: (S["h1"][s] if s < NS else None), True)
                P.barrier()
            if upto >= 2:
                inproj()
                P.barrier()
            if upto >= 3:
                retention()
                P.barrier()
            if upto >= 4:
                attention()
                P.barrier()
            if upto >= 5:
                merge()
                P.barrier()
            if upto >= 6:
                ffn(S["h2"], I["wgu2"], I["wdn2"], 2, own, lambda s: outT[s], False)

        Pd = Prog(nc, dry=True)
        wsd = WStream(Pd, WR)
        run_pass(Pd, wsd)
        Pr = Prog(nc)
        wsr = WStream(Pr, WR, plan=wsd.req)
        run_pass(Pr, wsr)
        assert wsr.k == len(wsd.req)
        Pr.emit()
        build.stats = Pr.stats
    return nc


_CACHE = {}


def run_cfg(inputs, cfg, upto=99, dbg=False):
    inp = {k: np.asarray(v, dtype=np.float32) for k, v in inputs.items()}
    per_core = prep_host(inp, cfg)
    key = (cfg.D, cfg.DFF, cfg.SEQ, upto, dbg)
    if key not in _CACHE:
        _CACHE[key] = build(cfg, upto, dbg)
    nc = _CACHE[key]
    res = run_bass_kernel_spmd(nc, per_core, core_ids=list(range(8)))
    out = np.empty((cfg.B, cfg.SEQ, cfg.D), np.float32)
    for core in range(8):
        b, j = core // 4, core % 4
        o = res.results[core]["outT"]
        o = o.transpose(0, 3, 1, 2).reshape(cfg.T, cfg.D)
        out[b, j * cfg.T:(j + 1) * cfg.T] = o
    return out, res


def kernel(**inputs):
    out, _ = run_cfg(inputs, Cfg())
    return out
```
